# Optimizing a Trainium2 kernel written in Bass

```python
import math
import jax
import jax.numpy as jnp
from jax import lax
import numpy as np

D_MODEL = 1024
BATCH = 16
SEQ = 4096
DEPTH = 4

GRID_W = 64
CTX_LEN = 256
N_MIXERS = 4
QBLOCK = 128
ROPE_BASE = 10000.0
LN_EPS = 1e-5
RMS_EPS = 1e-6
NEG_INF = -1e30
DEEPNORM_ALPHA = (2.0 * DEPTH) ** 0.25
DEEPNORM_BETA = (8.0 * DEPTH) ** -0.25

MLA_HEADS = 16
MLA_NOPE_DIM = 64
MLA_ROPE_DIM = 32
MLA_V_DIM = 64
MLA_Q_LORA = 384
MLA_KV_LORA = 256

NA_HEADS = 16
NA_HEAD_DIM = 64
NA_ROWS = 8
NA_COLS = 16
NA_QCOLS = 16
NA_KCOLS = 32

DIFF_HEADS = 8
DIFF_HEAD_DIM = 64

SWA_HEADS = 16
SWA_KV_HEADS = 4
SWA_HEAD_DIM = 64
SWA_WINDOW = 128

FFN_HIDDEN = 2816
FFN_CONV = 3

kernel_name = 'hybrid_diffusion_prefix_trunk'


def layer_norm(x, g, b):
    xf = x.astype(jnp.float32)
    mu = jnp.mean(xf, -1, keepdims=True)
    var = jnp.mean(jnp.square(xf - mu), -1, keepdims=True)
    return ((xf - mu) * lax.rsqrt(var + LN_EPS)).astype(x.dtype) * g + b


def rms_norm(x, g):
    xf = x.astype(jnp.float32)
    return (xf * lax.rsqrt(jnp.mean(jnp.square(xf), -1, keepdims=True) + RMS_EPS)).astype(x.dtype) * g


def softmax_f32(s):
    return jax.nn.softmax(s.astype(jnp.float32), axis=-1)


def modulate(t, shift, scale):
    return t * (1.0 + scale) + shift


def axial_rope(n_tokens, rot_dim):
    t = jnp.arange(n_tokens)
    row = (t // GRID_W).astype(jnp.float32)
    col = (t % GRID_W).astype(jnp.float32)
    n_freq = rot_dim // 4
    inv_freq = ROPE_BASE ** (-jnp.arange(n_freq, dtype=jnp.float32) / n_freq)
    ang = jnp.concatenate([row[:, None] * inv_freq, col[:, None] * inv_freq], -1)
    return jnp.cos(ang), jnp.sin(ang)


def apply_rope(x, cos, sin):
    half = x.shape[-1] // 2
    shp = (cos.shape[0],) + (1,) * (x.ndim - 3) + (half,)
    c = cos.reshape(shp).astype(x.dtype)
    s = sin.reshape(shp).astype(x.dtype)
    x1, x2 = x[..., :half], x[..., half:]
    return jnp.concatenate([x1 * c - x2 * s, x2 * c + x1 * s], -1)


def ctx_attention(q, k, v):
    s = jnp.einsum('bqhd,bkhd->bhqk', q, k).astype(jnp.float32) * q.shape[-1] ** -0.5
    p = softmax_f32(s).astype(v.dtype)
    return jnp.einsum('bhqk,bkhd->bqhd', p, v)


def dense_attention(q_lat, q_ctx, k_lat, v_lat, k_ctx, v_ctx):
    B, S, H, dk = q_lat.shape
    dv = v_lat.shape[-1]
    n_ctx = k_ctx.shape[1]
    scale = dk ** -0.5
    nb = S // QBLOCK

    def to_blocks(t):
        return t.reshape(B, nb, QBLOCK, H, dk).swapaxes(0, 1)

    def one_block(qs):
        ql, qc = qs
        s = jnp.concatenate([jnp.einsum('bqhd,bkhd->bhqk', qc, k_ctx),
                             jnp.einsum('bqhd,bkhd->bhqk', ql, k_lat)], -1)
        p = softmax_f32(s.astype(jnp.float32) * scale).astype(v_lat.dtype)
        return (jnp.einsum('bhqk,bkhd->bqhd', p[..., :n_ctx], v_ctx)
                + jnp.einsum('bhqk,bkhd->bqhd', p[..., n_ctx:], v_lat))

    out = lax.map(one_block, (to_blocks(q_lat), to_blocks(q_ctx)))
    return out.swapaxes(0, 1).reshape(B, S, H, dv)


def mla_mixer(h, hc, w_in, q_norm, kv_norm, w_qb, w_kvb, w_o, need_ctx):
    H, DN, DR, DV, QL = MLA_HEADS, MLA_NOPE_DIM, MLA_ROPE_DIM, MLA_V_DIM, MLA_Q_LORA

    def project_q(t):
        cq = rms_norm(t @ w_in[:, :QL], q_norm)
        return (cq @ w_qb).reshape(t.shape[0], t.shape[1], H, DN + DR)

    def project_kv(t):
        a = t @ w_in[:, QL:]
        ckv = rms_norm(a[..., :MLA_KV_LORA], kv_norm)
        kv = (ckv @ w_kvb).reshape(t.shape[0], t.shape[1], H, DN + DV)
        return kv[..., :DN], kv[..., DN:], a[..., MLA_KV_LORA:]

    def full_key(k_nope, k_pe):
        return jnp.concatenate([k_nope, jnp.broadcast_to(k_pe[:, :, None, :], k_nope.shape[:3] + (DR,))], -1)

    B, S, _ = h.shape
    cos, sin = axial_rope(S, DR)
    q = project_q(h)
    k_nope, v, k_pe = project_kv(h)
    q_rot = jnp.concatenate([q[..., :DN], apply_rope(q[..., DN:], cos, sin)], -1)
    k = full_key(k_nope, apply_rope(k_pe, cos, sin))
    kc_nope, vc, kc_pe = project_kv(hc)
    kc = full_key(kc_nope, kc_pe)
    o = dense_attention(q_rot, q, k, v, kc, vc)
    y = o.reshape(B, S, H * DV) @ w_o
    yc = None
    if need_ctx:
        oc = ctx_attention(project_q(hc), kc, vc)
        yc = oc.reshape(oc.shape[0], oc.shape[1], H * DV) @ w_o
    return y, yc


def neighbourhood_attention(q, k, v, kc, vc, rpb):
    B, S, H, d = q.shape
    n_rows = S // GRID_W
    kr = min(NA_ROWS, n_rows)
    n_ctx = kc.shape[1]
    scale = d ** -0.5
    n_cb = GRID_W // NA_QCOLS
    q_cols = np.arange(GRID_W).reshape(n_cb, NA_QCOLS)
    win_start = np.clip(q_cols - NA_COLS // 2, 0, GRID_W - NA_COLS)
    band_start = np.clip(np.arange(n_cb) * NA_QCOLS - NA_COLS // 2, 0, GRID_W - NA_KCOLS)
    band_cols = band_start[:, None] + np.arange(NA_KCOLS)
    col_valid = ((band_cols[:, None, :] >= win_start[..., None])
                 & (band_cols[:, None, :] < win_start[..., None] + NA_COLS))
    col_idx = np.clip(band_cols[:, None, :] - q_cols[..., None] + NA_COLS - 1, 0, 2 * NA_COLS - 2)
    qg = q.reshape(B, n_rows, GRID_W, H, d)
    kg = k.reshape(B, n_rows, GRID_W, H, d)
    vg = v.reshape(B, n_rows, GRID_W, H, d)

    def row_block(r):
        r0 = jnp.clip(r - kr // 2, 0, n_rows - kr)
        kb = lax.dynamic_slice_in_dim(kg, r0, kr, axis=1)[:, :, band_cols]
        vb = lax.dynamic_slice_in_dim(vg, r0, kr, axis=1)[:, :, band_cols]
        qr = lax.dynamic_index_in_dim(qg, r, axis=1, keepdims=False).reshape(B, n_cb, NA_QCOLS, H, d)
        s = jnp.einsum('bjqhd,bmjnhd->bhjqmn', qr, kb).astype(jnp.float32) * scale
        row_idx = r0 + jnp.arange(kr) - r + NA_ROWS - 1
        bias = rpb[:, row_idx][:, :, col_idx].transpose(0, 2, 3, 1, 4)
        s = jnp.where(col_valid[:, :, None, :], s + bias.astype(jnp.float32), NEG_INF)
        s_ctx = jnp.einsum('bjqhd,bkhd->bhjqk', qr, kc).astype(jnp.float32) * scale
        p = softmax_f32(jnp.concatenate(
            [s_ctx, s.reshape(B, H, n_cb, NA_QCOLS, kr * NA_KCOLS)], -1)).astype(v.dtype)
        p_lat = p[..., n_ctx:].reshape(B, H, n_cb, NA_QCOLS, kr, NA_KCOLS)
        o = (jnp.einsum('bhjqk,bkhd->bjqhd', p[..., :n_ctx], vc)
             + jnp.einsum('bhjqmn,bmjnhd->bjqhd', p_lat, vb))
        return o.reshape(B, GRID_W, H, d)

    out = lax.map(row_block, jnp.arange(n_rows))
    return out.swapaxes(0, 1).reshape(B, S, H, d)


def na_mixer(h, hc, w_qkv, rpb, w_o, need_ctx):
    H, d = NA_HEADS, NA_HEAD_DIM
    nq = H * d

    def project_q(t):
        return (t @ w_qkv[:, :nq]).reshape(t.shape[0], t.shape[1], H, d)

    def project_kv(t):
        a = (t @ w_qkv[:, nq:]).reshape(t.shape[0], t.shape[1], 2, H, d)
        return a[:, :, 0], a[:, :, 1]

    B, S, _ = h.shape
    q = project_q(h)
    k, v = project_kv(h)
    kc, vc = project_kv(hc)
    y = neighbourhood_attention(q, k, v, kc, vc, rpb).reshape(B, S, H * d) @ w_o
    yc = None
    if need_ctx:
        oc = ctx_attention(project_q(hc), kc, vc)
        yc = oc.reshape(oc.shape[0], oc.shape[1], H * d) @ w_o
    return y, yc


def diff_mixer(h, hc, w_qkv, lam, subln, w_o, layer_idx, need_ctx):
    H, d = DIFF_HEADS, DIFF_HEAD_DIM
    nq = 2 * H * d

    def project_q(t):
        return (t @ w_qkv[:, :nq]).reshape(t.shape[0], t.shape[1], H, 2, d)

    def project_kv(t):
        a = t @ w_qkv[:, nq:]
        return (a[..., :nq].reshape(t.shape[0], t.shape[1], H, 2, d),
                a[..., nq:].reshape(t.shape[0], t.shape[1], H, 2 * d))

    lambda_init = 0.8 - 0.6 * math.exp(-0.3 * layer_idx)
    lf = lam.astype(jnp.float32)
    lam_full = jnp.exp(jnp.sum(lf[0] * lf[1])) - jnp.exp(jnp.sum(lf[2] * lf[3])) + lambda_init

    def combine(o1, o2):
        o = o1 - lam_full.astype(o1.dtype) * o2
        o = rms_norm(o, subln) * (1.0 - lambda_init)
        return o.reshape(o.shape[0], o.shape[1], H * 2 * d) @ w_o

    B, S, _ = h.shape
    cos, sin = axial_rope(S, d)
    q = project_q(h)
    k, v = project_kv(h)
    q_rot, k_rot = apply_rope(q, cos, sin), apply_rope(k, cos, sin)
    kc, vc = project_kv(hc)
    o1 = dense_attention(q_rot[:, :, :, 0], q[:, :, :, 0], k_rot[:, :, :, 0], v, kc[:, :, :, 0], vc)
    o2 = dense_attention(q_rot[:, :, :, 1], q[:, :, :, 1], k_rot[:, :, :, 1], v, kc[:, :, :, 1], vc)
    y = combine(o1, o2)
    yc = None
    if need_ctx:
        qc = project_q(hc)
        yc = combine(ctx_attention(qc[:, :, :, 0], kc[:, :, :, 0], vc),
                     ctx_attention(qc[:, :, :, 1], kc[:, :, :, 1], vc))
    return y, yc


def windowed_sink_attention(q_lat, q_ctx, k_lat, v_lat, k_ctx, v_ctx, sinks):
    B, S, G, R, d = q_lat.shape
    n_ctx = k_ctx.shape[1]
    scale = d ** -0.5
    nb = S // QBLOCK
    band = QBLOCK + 2 * SWA_WINDOW
    pad = ((0, 0), (SWA_WINDOW, SWA_WINDOW), (0, 0), (0, 0))
    kp = jnp.pad(k_lat, pad)
    vp = jnp.pad(v_lat, pad)
    sink_logit = sinks.astype(jnp.float32).reshape(G, R)[None, :, :, None, None]

    def one_block(n):
        start = n * QBLOCK
        ql = lax.dynamic_slice_in_dim(q_lat, start, QBLOCK, axis=1)
        qc = lax.dynamic_slice_in_dim(q_ctx, start, QBLOCK, axis=1)
        kb = lax.dynamic_slice_in_dim(kp, start, band, axis=1)
        vb = lax.dynamic_slice_in_dim(vp, start, band, axis=1)
        qpos = start + jnp.arange(QBLOCK)
        kpos = start - SWA_WINDOW + jnp.arange(band)
        valid = ((jnp.abs(qpos[:, None] - kpos[None, :]) <= SWA_WINDOW)
                 & (kpos[None, :] >= 0) & (kpos[None, :] < S))
        s_lat = jnp.where(valid, jnp.einsum('bqgrd,bkgd->bgrqk', ql, kb).astype(jnp.float32) * scale, NEG_INF)
        s_ctx = jnp.einsum('bqgrd,bkgd->bgrqk', qc, k_ctx).astype(jnp.float32) * scale
        sink = jnp.broadcast_to(sink_logit, s_ctx.shape[:-1] + (1,))
        p = softmax_f32(jnp.concatenate([sink, s_ctx, s_lat], -1))[..., 1:].astype(v_lat.dtype)
        return (jnp.einsum('bgrqk,bkgd->bqgrd', p[..., :n_ctx], v_ctx)
                + jnp.einsum('bgrqk,bkgd->bqgrd', p[..., n_ctx:], vb))

    out = lax.map(one_block, jnp.arange(nb))
    return out.swapaxes(0, 1).reshape(B, S, G * R * d)


def ctx_sink_attention(q, k, v, sinks):
    B, L, G, R, d = q.shape
    s = jnp.einsum('bqgrd,bkgd->bgrqk', q, k).astype(jnp.float32) * d ** -0.5
    sink = jnp.broadcast_to(sinks.astype(jnp.float32).reshape(G, R)[None, :, :, None, None], s.shape[:-1] + (1,))
    p = softmax_f32(jnp.concatenate([sink, s], -1))[..., 1:].astype(v.dtype)
    return jnp.einsum('bgrqk,bkgd->bqgrd', p, v).reshape(B, L, G * R * d)


def swa_mixer(h, hc, w_qkv, sinks, w_o, need_ctx):
    H, G, d = SWA_HEADS, SWA_KV_HEADS, SWA_HEAD_DIM
    R = H // G
    nq = H * d

    def project_q(t):
        return (t @ w_qkv[:, :nq]).reshape(t.shape[0], t.shape[1], G, R, d)

    def project_kv(t):
        a = t @ w_qkv[:, nq:]
        return (a[..., :G * d].reshape(t.shape[0], t.shape[1], G, d),
                a[..., G * d:].reshape(t.shape[0], t.shape[1], G, d))

    B, S, _ = h.shape
    cos, sin = axial_rope(S, d)
    q = project_q(h)
    k, v = project_kv(h)
    kc, vc = project_kv(hc)
    o = windowed_sink_attention(apply_rope(q, cos, sin), q, apply_rope(k, cos, sin), v, kc, vc, sinks)
    y = o @ w_o
    yc = None
    if need_ctx:
        yc = ctx_sink_attention(project_q(hc), kc, vc, sinks) @ w_o
    return y, yc


def conv_ffn(h, w_in, conv_w, conv_b, w_out):
    u = h @ w_in
    u = lax.conv_general_dilated(
        u, conv_w[:, None, :].astype(u.dtype), window_strides=(1,),
        padding=[(FFN_CONV // 2, FFN_CONV // 2)],
        dimension_numbers=('NWC', 'WIO', 'NWC'),
        feature_group_count=u.shape[-1]) + conv_b
    a, g = jnp.split(u, 2, axis=-1)
    return (jax.nn.silu(g) * a) @ w_out


def setup_inputs(seed: int = 0) -> dict:
    key = jax.random.key(seed)
    ks = iter(jax.random.split(key, 40))

    def nrm(shape, scale):
        return jax.random.normal(next(ks), shape, jnp.float32) * scale

    D, F = D_MODEL, FFN_HIDDEN
    beta = DEEPNORM_BETA
    n_a, n_b, n_c, n_d = [len(range(m, DEPTH, N_MIXERS)) for m in range(N_MIXERS)]
    mla_in = MLA_Q_LORA + MLA_KV_LORA + MLA_ROPE_DIM
    return {
        'x': nrm((BATCH, SEQ, D), 1.0),
        'c': nrm((BATCH, D), 1.0),
        'ctx': nrm((BATCH, CTX_LEN, D), 1.0),
        'c_ctx': nrm((D,), 1.0),
        'ada_w': nrm((DEPTH, D, 6 * D), 0.5 * D ** -0.5),
        'ada_b': nrm((DEPTH, 6 * D), 0.01),
        'ln1_g': 1.0 + nrm((DEPTH, D), 0.02),
        'ln1_b': nrm((DEPTH, D), 0.02),
        'ln2_g': 1.0 + nrm((DEPTH, D), 0.02),
        'ln2_b': nrm((DEPTH, D), 0.02),
        'ffn_w_in': nrm((DEPTH, D, 2 * F), D ** -0.5),
        'ffn_conv_w': nrm((DEPTH, FFN_CONV, 2 * F), FFN_CONV ** -0.5),
        'ffn_conv_b': nrm((DEPTH, 2 * F), 0.01),
        'ffn_w_out': nrm((DEPTH, F, D), beta * F ** -0.5),
        'mla_w_in': nrm((n_a, D, mla_in), D ** -0.5),
        'mla_q_norm': 1.0 + nrm((n_a, MLA_Q_LORA), 0.02),
        'mla_kv_norm': 1.0 + nrm((n_a, MLA_KV_LORA), 0.02),
        'mla_w_qb': nrm((n_a, MLA_Q_LORA, MLA_HEADS * (MLA_NOPE_DIM + MLA_ROPE_DIM)), MLA_Q_LORA ** -0.5),
        'mla_w_kvb': nrm((n_a, MLA_KV_LORA, MLA_HEADS * (MLA_NOPE_DIM + MLA_V_DIM)), MLA_KV_LORA ** -0.5),
        'mla_w_o': nrm((n_a, MLA_HEADS * MLA_V_DIM, D), beta * (MLA_HEADS * MLA_V_DIM) ** -0.5),
        'na_w_qkv': nrm((n_b, D, 3 * NA_HEADS * NA_HEAD_DIM), D ** -0.5),
        'na_rpb': nrm((n_b, NA_HEADS, 2 * NA_ROWS - 1, 2 * NA_COLS - 1), 0.1),
        'na_w_o': nrm((n_b, NA_HEADS * NA_HEAD_DIM, D), beta * (NA_HEADS * NA_HEAD_DIM) ** -0.5),
        'diff_w_qkv': nrm((n_c, D, 6 * DIFF_HEADS * DIFF_HEAD_DIM), D ** -0.5),
        'diff_lambda': nrm((n_c, 4, DIFF_HEAD_DIM), 0.1),
        'diff_subln': 1.0 + nrm((n_c, 2 * DIFF_HEAD_DIM), 0.02),
        'diff_w_o': nrm((n_c, DIFF_HEADS * 2 * DIFF_HEAD_DIM, D), beta * (DIFF_HEADS * 2 * DIFF_HEAD_DIM) ** -0.5),
        'swa_w_qkv': nrm((n_d, D, (SWA_HEADS + 2 * SWA_KV_HEADS) * SWA_HEAD_DIM), D ** -0.5),
        'swa_sinks': nrm((n_d, SWA_HEADS), 0.5),
        'swa_w_o': nrm((n_d, SWA_HEADS * SWA_HEAD_DIM, D), beta * (SWA_HEADS * SWA_HEAD_DIM) ** -0.5),
    }


def reference(x, c, ctx, c_ctx, ada_w, ada_b, ln1_g, ln1_b, ln2_g, ln2_b,
              ffn_w_in, ffn_conv_w, ffn_conv_b, ffn_w_out,
              mla_w_in, mla_q_norm, mla_kv_norm, mla_w_qb, mla_w_kvb, mla_w_o,
              na_w_qkv, na_rpb, na_w_o,
              diff_w_qkv, diff_lambda, diff_subln, diff_w_o,
              swa_w_qkv, swa_sinks, swa_w_o):
    silu_c = jax.nn.silu(c)
    silu_cc = jax.nn.silu(c_ctx)
    h_lat, h_ctx = x, ctx
    for i in range(DEPTH):
        last = i == DEPTH - 1
        need_ctx = not last
        m = (silu_c @ ada_w[i] + ada_b[i])[:, None, :]
        sh1, sc1, g1, sh2, sc2, g2 = jnp.split(m, 6, axis=-1)
        mc = silu_cc @ ada_w[i] + ada_b[i]
        csh1, csc1, cg1, csh2, csc2, cg2 = jnp.split(mc, 6, axis=-1)
        a_lat = modulate(h_lat, sh1, sc1)
        a_ctx = modulate(h_ctx, csh1, csc1)
        kind, slot = i % N_MIXERS, i // N_MIXERS
        if kind == 0:
            y, yc = mla_mixer(a_lat, a_ctx, mla_w_in[slot], mla_q_norm[slot], mla_kv_norm[slot],
                              mla_w_qb[slot], mla_w_kvb[slot], mla_w_o[slot], need_ctx)
        elif kind == 1:
            y, yc = na_mixer(a_lat, a_ctx, na_w_qkv[slot], na_rpb[slot], na_w_o[slot], need_ctx)
        elif kind == 2:
            y, yc = diff_mixer(a_lat, a_ctx, diff_w_qkv[slot], diff_lambda[slot], diff_subln[slot],
                               diff_w_o[slot], i, need_ctx)
        else:
            y, yc = swa_mixer(a_lat, a_ctx, swa_w_qkv[slot], swa_sinks[slot], swa_w_o[slot], need_ctx)
        h_lat = layer_norm(DEEPNORM_ALPHA * h_lat + g1 * y, ln1_g[i], ln1_b[i])
        f = conv_ffn(modulate(h_lat, sh2, sc2), ffn_w_in[i], ffn_conv_w[i], ffn_conv_b[i], ffn_w_out[i])
        h_lat = layer_norm(DEEPNORM_ALPHA * h_lat + g2 * f, ln2_g[i], ln2_b[i])
        if need_ctx:
            h_ctx = layer_norm(DEEPNORM_ALPHA * h_ctx + cg1 * yc, ln1_g[i], ln1_b[i])
            fc = conv_ffn(modulate(h_ctx, csh2, csc2), ffn_w_in[i], ffn_conv_w[i], ffn_conv_b[i], ffn_w_out[i])
            h_ctx = layer_norm(DEEPNORM_ALPHA * h_ctx + cg2 * fc, ln2_g[i], ln2_b[i])
    return h_lat
```

```python
import math
from contextlib import ExitStack
import numpy as np
import concourse.bass as bass
import concourse.mybir as mybir
from concourse.bass_utils import run_bass_kernel_spmd

F32 = mybir.dt.float32
BF16 = mybir.dt.bfloat16
AF = mybir.ActivationFunctionType
ALU = mybir.AluOpType

NCORES = 8
NB = 2
LS = 4096
CS = 256
D = 1024
NK = LS + CS
FH = 2816
DEPTH = 4
ALPHA = (2.0 * DEPTH) ** 0.25
LN_EPS = 1e-5
RMS_EPS = 1e-6
NEG = -30000.0
SAME_SYNC = True
ENG = ('pe', 'act', 'dve', 'pool', 'sp')


class Sched:
    def __init__(self, nc, es):
        self.nc = nc
        self.es = es
        self.eobj = dict(pe=nc.tensor, act=nc.scalar, dve=nc.vector, pool=nc.gpsimd, sp=nc.sync)
        self.semh = {}
        self.semv = {}
        for e in ENG:
            self.semh[e] = es.enter_context(nc.semaphore('s_' + e))
            self.semv[e] = 0
        self.known = {e: {} for e in ENG}
        self.lastw = {}
        self.readers = {}
        self.ninstr = 0
        self.trace = None

    def _sem(self, key):
        if not hasattr(self, 'pmap'):
            self.pmap = {}
            self.npool = 0
        if key not in self.pmap:
            idx = len(self.pmap)
            phys = f'D{idx}'
            if phys not in self.semh:
                self.semh[phys] = self.es.enter_context(self.nc.semaphore('d_' + phys))
                self.semv[phys] = 0
            self.pmap[key] = phys
        return self.pmap[key]

    def _wait(self, e, ev):
        sk, val = ev
        if sk == e and (e == 'pe' or not SAME_SYNC):
            return
        if self.known[e].get(sk, 0) >= val:
            return
        self.eobj[e].wait_ge(self.semh[sk], val)
        self.known[e][sk] = val
        self.ninstr += 1
        if self.trace is not None:
            self.trace.append((e, 'w', sk, val))

    def _deps(self, e, reads, writes):
        for r in reads:
            ev = self.lastw.get(r)
            if ev:
                self._wait(e, ev)
        for w in writes:
            ev = self.lastw.get(w)
            if ev:
                self._wait(e, ev)
            for sk, val in self.readers.get(w, {}).items():
                if sk == e:
                    continue
                self._wait(e, (sk, val))

    def _record(self, ev, reads, writes):
        for r in reads:
            d = self.readers.setdefault(r, {})
            d[ev[0]] = max(d.get(ev[0], 0), ev[1])
        for w in writes:
            self.lastw[w] = ev
            self.readers[w] = {}

    def op(self, e, fn, reads=(), writes=()):
        self._deps(e, reads, writes)
        ins = fn(self.eobj[e])
        self.semv[e] += 1
        ins.then_inc(self.semh[e], 1)
        self._record((e, self.semv[e]), reads, writes)
        self.ninstr += 1
        if self.trace is not None:
            self.trace.append((e, 'i', e, 1))

    def dma(self, e, out, in_, reads=(), writes=(), sem='dma', **kw):
        sem = self._sem(sem)
        self._deps(e, reads, writes)
        ins = self.eobj[e].dma_start(out=out, in_=in_, **kw)
        self.semv[sem] += 16
        ins.then_inc(self.semh[sem], 16)
        self._record((sem, self.semv[sem]), reads, writes)
        self.ninstr += 1
        if self.trace is not None:
            self.trace.append((e, 'i', sem, 16))

    def barrier(self):
        for e in ENG:
            for sk in list(self.semh.keys()):
                if sk == e:
                    continue
                if self.semv[sk] > 0:
                    self._wait(e, (sk, self.semv[sk]))
        self.pmap = {}


def _rope_tables(rot_dim, prow0):
    t = np.arange(LS)
    row = (t // 64).astype(np.float32)
    col = (t % 64).astype(np.float32)
    n_freq = rot_dim // 4
    inv = (np.float32(10000.0) ** (-np.arange(n_freq, dtype=np.float32) / np.float32(n_freq))).astype(np.float32)
    ang = np.concatenate([row[:, None] * inv, col[:, None] * inv], -1).astype(np.float32)
    c = np.cos(ang).astype(np.float32).T
    s = np.sin(ang).astype(np.float32).T
    C = np.zeros((128, LS), np.float32)
    S = np.zeros((128, LS), np.float32)
    half = rot_dim // 2
    p = prow0
    while p + rot_dim <= 128:
        C[p:p + half] = c
        C[p + half:p + rot_dim] = c
        S[p:p + half] = -s
        S[p + half:p + rot_dim] = s
        p += rot_dim
        if prow0 != 0:
            break
    return C, S


def _na_tables(rpb):
    classes = [(0, kt) for kt in range(6)] + [(1, 2 + kk) for kk in range(8)] + [(7, kt) for kt in range(26, 32)]
    out = np.empty((16, 20, 128, 512), np.float32)
    p = np.arange(128)
    i = np.arange(512)
    for ti, (R, kt) in enumerate(classes):
        kr = (2 * kt + p // 64)[:, None]
        kc = (p % 64)[:, None]
        r = (8 * R + i // 64)[None, :]
        c = (i % 64)[None, :]
        r0 = np.clip(r - 4, 0, 56)
        ws = np.clip(c - 8, 0, 48)
        valid = (kr >= r0) & (kr < r0 + 8) & (kc >= ws) & (kc < ws + 16)
        dr = np.clip(kr - r + 7, 0, 14)
        dc = np.clip(kc - c + 15, 0, 30)
        g = rpb[:, dr, dc]
        out[:, ti] = np.where(valid[None], g, np.float32(NEG))
    return out


def _na_tile_list(R):
    if R == 0:
        return [(kt, kt) for kt in range(6)]
    if R == 7:
        return [(kt, 14 + kt - 26) for kt in range(26, 32)]
    return [(4 * R - 2 + kk, 6 + kk) for kk in range(8)]


def _swa_masks():
    out = np.empty((6, 128, 512), np.float32)
    j = np.arange(128)[:, None]
    i = np.arange(512)[None, :]
    for t, kk in enumerate(range(-1, 5)):
        valid = np.abs(i - (128 * kk + j)) <= 128
        out[t] = np.where(valid, np.float32(0.0), np.float32(NEG))
    return out


def _swap_cols(ncols, hd, rot0, rot):
    idx = np.arange(ncols)
    h = idx // hd
    dd = idx % hd
    inrot = (dd >= rot0) & (dd < rot0 + rot)
    sw = rot0 + (dd - rot0 + rot // 2) % rot
    return np.where(inrot, h * hd + sw, idx)


def _prep_weights(inp):
    W = {}
    for i in range(DEPTH):
        W[f'w1_{i}'] = inp['ffn_w_in'][i]
        W[f'w2_{i}'] = inp['ffn_w_out'][i]
    w_in = inp['mla_w_in'][0]
    W['mla_cq'] = w_in[:, :384]
    W['mla_ckv'] = w_in[:, 384:640]
    kp = np.concatenate([w_in[:, 384:448], w_in[:, 640:672]], 1)
    kpsw = np.concatenate([w_in[:, 384:448], w_in[:, 640 + (np.arange(32) + 16) % 32]], 1)
    W['mla_kpe'] = np.concatenate([kp, kpsw], 1)
    wqb = inp['mla_w_qb'][0]
    W['mla_qb'] = np.concatenate([wqb, wqb[:, _swap_cols(1536, 96, 64, 32)]], 1)
    wkvb = inp['mla_w_kvb'][0].reshape(256, 16, 128)
    W['mla_kn'] = wkvb[:, :, :64].reshape(256, 1024)
    W['mla_v'] = wkvb[:, :, 64:].reshape(256, 1024)
    W['wo_0'] = inp['mla_w_o'][0]
    W['na_fm'] = inp['na_w_qkv'][0][:, :2048]
    W['na_v'] = inp['na_w_qkv'][0][:, 2048:]
    W['wo_1'] = inp['na_w_o'][0]
    dq = inp['diff_w_qkv'][0]
    W['diff_fm'] = np.concatenate([dq[:, :2048], dq[:, :2048][:, _swap_cols(2048, 64, 0, 64)]], 1)
    W['diff_v'] = dq[:, 2048:]
    W['wo_2'] = inp['diff_w_o'][0]
    sq = inp['swa_w_qkv'][0]
    W['swa_fm'] = np.concatenate([sq[:, :1280], sq[:, :1280][:, _swap_cols(1280, 64, 0, 64)]], 1)
    W['swa_v'] = sq[:, 1280:]
    W['wo_3'] = inp['swa_w_o'][0]
    return {k: np.ascontiguousarray(v, dtype=np.float32) for k, v in W.items()}


WSHAPES = {}
for _i in range(DEPTH):
    WSHAPES[f'w1_{_i}'] = (1024, 5632)
    WSHAPES[f'w2_{_i}'] = (2816, 1024)
    WSHAPES[f'wo_{_i}'] = (1024, 1024)
WSHAPES.update(mla_cq=(1024, 384), mla_ckv=(1024, 256), mla_kpe=(1024, 192), mla_qb=(384, 3072),
               mla_kn=(256, 1024), mla_v=(256, 1024), na_fm=(1024, 2048), na_v=(1024, 1024),
               diff_fm=(1024, 4096), diff_v=(1024, 1024), swa_fm=(1024, 2560), swa_v=(1024, 256))


def build(layers=(0, 1, 2, 3), dbg_ctx=False):
    nc = bass.Bass("TRN2", target_bir_lowering=False)
    es = ExitStack()
    S = Sched(nc, es)
    import os as _os0
    if _os0.environ.get('KTRACE'):
        S.trace = []

    def din(name, shape, dt=F32):
        return nc.dram_tensor(name, list(shape), dt, kind="ExternalInput").ap()

    def dscr(name, shape, dt):
        return nc.dram_tensor(name, list(shape), dt, kind="Internal").ap()

    x_in = din('x', (NB, LS, D))
    ctx_in = din('ctx', (NB, CS, D))
    c_in = din('c', (NB, D))
    cc_in = din('c_ctx', (D,))
    ada_w = din('ada_w', (DEPTH, D, 6 * D))
    ada_b = din('ada_b', (DEPTH, 6 * D))
    ln_in = {k: din(k, (DEPTH, D)) for k in ('ln1_g', 'ln1_b', 'ln2_g', 'ln2_b')}
    convw_in = din('ffn_conv_w', (DEPTH, 3, 2 * FH))
    convb_in = din('ffn_conv_b', (DEPTH, 2 * FH))
    qn_in = din('mla_q_norm', (1, 384))
    kvn_in = din('mla_kv_norm', (1, 256))
    lam_in = din('diff_lambda', (1, 4, 64))
    subln_in = din('diff_subln', (1, 128))
    sinks_in = din('swa_sinks', (1, 16))
    ident_in = din('ident', (128, 128))
    ropeC64_in = din('ropeC64', (128, LS))
    ropeS64_in = din('ropeS64', (128, LS))
    ropeC32_in = din('ropeC32', (128, LS))
    ropeS32_in = din('ropeS32', (128, LS))
    nabias_in = din('nabias', (16, 20, 128, 512))
    swamask_in = din('swamask', (6, 128, 512))
    GD = [dscr(f'GD{b}', (FH, NK), BF16) for b in range(NB)]
    Wf = {k: din('wf_' + k, shp) for k, shp in WSHAPES.items()}
    Wb = {k: dscr('wb_' + k, shp, BF16) for k, shp in WSHAPES.items()}
    out_d = nc.dram_tensor('out', [NB, LS, D], F32, kind="ExternalOutput").ap()
    if dbg_ctx:
        outc_d = nc.dram_tensor('outc', [NB, CS, D], F32, kind="ExternalOutput").ap()

    hL = [dscr(f'hL{b}', (D, LS), F32) for b in range(NB)]
    hC = [dscr(f'hC{b}', (D, CS), F32) for b in range(NB)]
    QRT = [dscr(f'QRT{b}', (1536, LS), BF16) for b in range(NB)]
    QPT = [dscr(f'QPT{b}', (1536, NK), BF16) for b in range(NB)]
    KTD = [dscr(f'KTD{b}', (1024, NK), BF16) for b in range(NB)]
    KPE = [dscr(f'KPE{b}', (32, NK), BF16) for b in range(NB)]
    VD = [dscr(f'VD{b}', (NK, 1024), BF16) for b in range(NB)]
    OTD = [dscr(f'OTD{b}', (1024, NK), BF16) for b in range(NB)]
    OD = [[dscr(f'OD{b}_{j}', (1024, NK), F32) for j in range(2)] for b in range(NB)]

    def hview(b, seg):
        t = hL[b] if seg == 'L' else hC[b]
        return t.rearrange("(c p) t -> p c t", p=128)

    def chunked(ap):
        return ap.rearrange("(c p) n -> p c n", p=128)

    uniq = [0]

    def sb(stack, name, shape, dt=F32):
        uniq[0] += 1
        return stack.enter_context(nc.sbuf_tensor(f'{name}_u{uniq[0]}', list(shape), dt))

    ps = [es.enter_context(nc.psum_tensor(f'ps{i}', [128, 512], F32)) for i in range(8)]
    PS = [f'ps{i}' for i in range(8)]

    onesf = sb(es, 'onesf', (128, 128))
    ident = sb(es, 'ident_sb', (128, 128))
    mod = sb(es, 'mod', (128, 3, 48))
    modp = sb(es, 'modp', (128, 3, 16))
    scT = sb(es, 'scT', (128, 3, 8))
    lnp = sb(es, 'lnp', (128, 4, 8))
    epsln = sb(es, 'epsln', (128, 1))
    epsrms = sb(es, 'epsrms', (128, 1))
    S.op('dve', lambda e: e.memset(onesf[:], 1.0), writes=['onesf'])
    S.op('dve', lambda e: e.memset(epsln[:], LN_EPS), writes=['eps'])
    S.op('dve', lambda e: e.memset(epsrms[:], RMS_EPS), writes=['eps'])
    S.dma('sp', ident[:], ident_in[:, :], writes=['ident'], sem='ldc')

    for k, shp in WSHAPES.items():
        for r0 in range(0, shp[0], 128):
            S.dma('pool', Wb[k][r0:r0 + 128, :], Wf[k][r0:r0 + 128, :], sem='wcast')

    with ExitStack() as ph:
        craw = sb(ph, 'craw', (128, 3, 8))
        with nc.allow_non_contiguous_dma("tiny transposed vector loads"):
            for j in range(3):
                src = c_in[j, :] if j < 2 else cc_in
                S.dma('sp', craw[:, j, :], src.rearrange("(k p) -> p k", p=128), writes=['craw'], sem='ldc')
        S.op('act', lambda e: e.activation(out=scT[:], in_=craw[:], func=AF.Silu), reads=['craw'], writes=['scT'])
        S.barrier()

    def transpose_in():
        with ExitStack() as ph:
            xt = [sb(ph, f'xt{i}', (128, 4, D)) for i in range(2)]
            ht = [sb(ph, f'htt{i}', (128, 8, 512)) for i in range(2)]
            n = 0
            for b in range(NB):
                for seg, src, Ls in (('L', x_in[b], LS), ('C', ctx_in[b], CS)):
                    Tn = min(512, Ls)
                    ns = Tn // 128
                    for t0 in range(0, Ls, Tn):
                        sl = n % 2
                        S.dma('sp', xt[sl][:, :ns, :], src[t0:t0 + Tn, :].rearrange("(s p) d -> p s d", p=128),
                              writes=[f'xt{sl}'], sem=f'ldx{sl}')
                        for ch in range(8):
                            bank = ch % 4
                            for s in range(ns):
                                S.op('pe', lambda e, s=s, ch=ch, bank=bank, sl=sl: e.transpose(
                                    ps[bank][:, s * 128:(s + 1) * 128], xt[sl][:, s, ch * 128:(ch + 1) * 128], ident[:]),
                                    reads=[f'xt{sl}', 'ident'], writes=[PS[bank]])
                            eng = 'act' if ch % 2 == 0 else 'dve'
                            if eng == 'act':
                                S.op('act', lambda e, ch=ch, bank=bank, sl=sl: e.activation(
                                    out=ht[sl][:, ch, :Tn], in_=ps[bank][:, :Tn], func=AF.Copy),
                                    reads=[PS[bank]], writes=[f'htt{sl}'])
                            else:
                                S.op('dve', lambda e, ch=ch, bank=bank, sl=sl: e.tensor_copy(
                                    out=ht[sl][:, ch, :Tn], in_=ps[bank][:, :Tn]),
                                    reads=[PS[bank]], writes=[f'htt{sl}'])
                        S.dma('pool', hview(b, seg)[:, :, t0:t0 + Tn], ht[sl][:, :, :Tn], reads=[f'htt{sl}'],
                              sem=f'stx{sl}')
                        n += 1
            S.barrier()

    def transpose_out():
        with ExitStack() as ph:
            ht = [sb(ph, f'oht{i}', (128, 8, 512)) for i in range(2)]
            yt = [sb(ph, f'oyt{i}', (128, 4, D)) for i in range(2)]
            n = 0
            segs = [('L', LS)] + ([('C', CS)] if dbg_ctx else [])
            for b in range(NB):
                for seg, Ls in segs:
                    dst = out_d[b] if seg == 'L' else outc_d[b]
                    Tn = min(512, Ls)
                    ns = Tn // 128
                    for t0 in range(0, Ls, Tn):
                        sl = n % 2
                        S.dma('sp', ht[sl][:, :, :Tn], hview(b, seg)[:, :, t0:t0 + Tn], writes=[f'oht{sl}'], sem=f'ldx{sl}')
                        for s in range(ns):
                            for half in range(2):
                                bank = (s * 2 + half) % 4
                                for c4 in range(4):
                                    ch = half * 4 + c4
                                    S.op('pe', lambda e, s=s, ch=ch, c4=c4, bank=bank, sl=sl: e.transpose(
                                        ps[bank][:, c4 * 128:(c4 + 1) * 128], ht[sl][:, ch, s * 128:(s + 1) * 128], ident[:]),
                                        reads=[f'oht{sl}', 'ident'], writes=[PS[bank]])
                                if half == 0:
                                    S.op('act', lambda e, s=s, bank=bank, sl=sl: e.activation(
                                        out=yt[sl][:, s, 0:512], in_=ps[bank][:, :], func=AF.Copy),
                                        reads=[PS[bank]], writes=[f'oyt{sl}'])
                                else:
                                    S.op('dve', lambda e, s=s, bank=bank, sl=sl: e.tensor_copy(
                                        out=yt[sl][:, s, 512:1024], in_=ps[bank][:, :]),
                                        reads=[PS[bank]], writes=[f'oyt{sl}'])
                        S.dma('pool', dst[t0:t0 + Tn, :].rearrange("(s p) d -> p s d", p=128), yt[sl][:, :ns, :],
                              reads=[f'oyt{sl}'], sem=f'stx{sl}')
                        n += 1
            S.barrier()

    def phase_mod(i):
        with ExitStack() as ph:
            aw = [sb(ph, f'aw{k}', (128, 8, 768)) for k in range(2)]
            adab = sb(ph, 'adab', (128, 48))
            with nc.allow_non_contiguous_dma("tiny transposed vector loads"):
                S.dma('sp', adab[:], ada_b[i, :].rearrange("(o p) -> p o", p=128), writes=['adab'], sem='ldc')
                for q, k in enumerate(('ln1_g', 'ln1_b', 'ln2_g', 'ln2_b')):
                    S.dma('sp', lnp[:, q, :], ln_in[k][i, :].rearrange("(o p) -> p o", p=128), writes=['lnp'], sem='ldc')
            for g in range(8):
                sl = g % 2
                S.dma('sp', aw[sl][:], chunked(ada_w[i])[:, :, g * 768:(g + 1) * 768], writes=[f'aw{sl}'], sem=f'ldaw{sl}')
                for o6 in range(6):
                    oc = g * 6 + o6
                    for k in range(8):
                        S.op('pe', lambda e, sl=sl, o6=o6, oc=oc, k=k: e.matmul(
                            ps[0][:, oc * 4:oc * 4 + 3], aw[sl][:, k, o6 * 128:(o6 + 1) * 128], scT[:, :, k],
                            start=(k == 0), stop=(k == 7)),
                            reads=[f'aw{sl}', 'scT'], writes=[PS[0]])
            pv = ps[0][:, 0:192].rearrange("p (o j) -> p o j", j=4)
            for j in range(3):
                S.op('dve', lambda e, j=j: e.tensor_tensor(out=mod[:, j, :], in0=pv[:, :, j], in1=adab[:], op=ALU.add),
                     reads=[PS[0], 'adab'], writes=['mod'])
            for j in range(3):
                S.op('dve', lambda e, j=j: e.tensor_scalar_add(out=modp[:, j, 0:8], in0=mod[:, j, 8:16], scalar1=1.0),
                     reads=['mod'], writes=['modp'])
                S.op('dve', lambda e, j=j: e.tensor_scalar_add(out=modp[:, j, 8:16], in0=mod[:, j, 32:40], scalar1=1.0),
                     reads=['mod'], writes=['modp'])
            S.barrier()

    def segs_for(need_ctx):
        out = []
        for b in range(NB):
            out.append((b, 'L', LS, b))
            if need_ctx:
                out.append((b, 'C', CS, 2))
        return out

    def colbase(seg):
        return 0 if seg == 'L' else LS

    def keybase(seg):
        return CS if seg == 'L' else 0

    def load_mod(hin, aT, sl, b, seg, t0, Tn, jm, which, acols=0):
        sh = mod[:, jm, 0:8] if which == 0 else mod[:, jm, 24:32]
        op1 = modp[:, jm, 0:8] if which == 0 else modp[:, jm, 8:16]
        S.dma('sp', hin[sl][:, :, :Tn], hview(b, seg)[:, :, t0:t0 + Tn], writes=[f'hin{sl}'], sem=f'ldh{sl}')
        aname = aT[1]
        for c in range(8):
            S.op('act', lambda e, c=c: e.activation(out=aT[0][:, c, acols:acols + Tn], in_=hin[sl][:, c, :Tn], func=AF.Identity,
                                                    bias=sh[:, c:c + 1], scale=op1[:, c:c + 1]),
                 reads=[f'hin{sl}', 'mod', 'modp'], writes=[aname])

    def phase_proj(i, kind, need_ctx):
        cfg = dict(na=dict(fm='na_fm', v='na_v', nq=8, nk=8, nv=1024, rope=False, sw0=0),
                   diff=dict(fm='diff_fm', v='diff_v', nq=8, nk=8, nv=1024, rope=True, sw0=2048),
                   swa=dict(fm='swa_fm', v='swa_v', nq=8, nk=2, nv=256, rope=True, sw0=1280))[kind]
        nfm = WSHAPES[cfg['fm']][1]
        nv = cfg['nv']
        with ExitStack() as ph:
            wfm = sb(ph, 'wfm', (128, 8, nfm), BF16)
            wv = sb(ph, 'wv', (128, 8, nv), BF16)
            S.dma('sp', wfm[:], chunked(Wb[cfg['fm']]), writes=['wfm'], sem='ldw')
            S.dma('sp', wv[:], chunked(Wb[cfg['v']]), writes=['wv'], sem='ldw')
            if cfg['rope']:
                rC = sb(ph, 'rC', (128, LS))
                rS = sb(ph, 'rS', (128, LS))
                for q0 in range(0, LS, 1024):
                    S.dma('sp', rC[:, q0:q0 + 1024], ropeC64_in[:, q0:q0 + 1024], writes=['rC'], sem='ldw')
                    S.dma('sp', rS[:, q0:q0 + 1024], ropeS64_in[:, q0:q0 + 1024], writes=['rS'], sem='ldw')
            hin = [sb(ph, f'hin{k}', (128, 8, 512)) for k in range(2)]
            aTs = [sb(ph, f'aT{k}', (128, 8, 512), BF16) for k in range(2)]
            stp = [sb(ph, f'stp{k}', (128, 512), BF16) for k in range(2)]
            strr = [sb(ph, f'str{k}', (128, 512), BF16) for k in range(2)]
            t1 = sb(ph, 't1', (128, 512))
            t2 = sb(ph, 't2', (128, 512))
            vst = [sb(ph, f'vst{k}', (128, 4, nv), BF16) for k in range(2)]
            n = 0
            npl = 0
            nrp = 0
            for (b, seg, Ls, jm) in segs_for(need_ctx):
                Tn = min(512, Ls)
                for t0 in range(0, Ls, Tn):
                    sl = n % 2
                    n += 1
                    load_mod(hin, (aTs[sl], f'aT{sl}'), sl, b, seg, t0, Tn, jm, 0)
                    aT = aTs[sl]
                    an = f'aT{sl}'
                    for oc in range(cfg['nq'] + cfg['nk']):
                        isq = oc < cfg['nq']
                        row0 = oc * 128 if isq else (oc - cfg['nq']) * 128
                        do_rope = cfg['rope'] and seg == 'L'
                        pb = (oc % 2) * 2
                        for k in range(8):
                            S.op('pe', lambda e, k=k, oc=oc, pb=pb: e.matmul(
                                ps[pb][:, :Tn], wfm[:, k, oc * 128:(oc + 1) * 128], aT[:, k, :Tn], start=(k == 0), stop=(k == 7)),
                                reads=['wfm', an], writes=[PS[pb]])
                        if do_rope:
                            for k in range(8):
                                S.op('pe', lambda e, k=k, oc=oc, pb=pb: e.matmul(
                                    ps[pb + 1][:, :Tn], wfm[:, k, cfg['sw0'] + oc * 128:cfg['sw0'] + (oc + 1) * 128], aT[:, k, :Tn],
                                    start=(k == 0), stop=(k == 7)),
                                    reads=['wfm', an], writes=[PS[pb + 1]])
                        need_plain = isq or (not do_rope)
                        if need_plain:
                            s2 = npl % 2
                            npl += 1
                            if do_rope:
                                S.op('dve', lambda e, pb=pb, s2=s2: e.tensor_copy(out=stp[s2][:, :Tn], in_=ps[pb][:, :Tn]),
                                     reads=[PS[pb]], writes=[f'stp{s2}'])
                            else:
                                S.op('act', lambda e, pb=pb, s2=s2: e.activation(out=stp[s2][:, :Tn], in_=ps[pb][:, :Tn], func=AF.Copy),
                                     reads=[PS[pb]], writes=[f'stp{s2}'])
                            if isq:
                                dst = QPT[b][row0:row0 + 128, colbase(seg) + t0:colbase(seg) + t0 + Tn]
                            else:
                                dst = KTD[b][row0:row0 + 128, keybase(seg) + t0:keybase(seg) + t0 + Tn]
                            S.dma('pool', dst, stp[s2][:, :Tn], reads=[f'stp{s2}'], sem=f'stp{s2}')
                        if do_rope:
                            s2 = nrp % 2
                            nrp += 1
                            S.op('dve', lambda e, pb=pb: e.tensor_tensor(out=t1[:, :Tn], in0=ps[pb][:, :Tn], in1=rC[:, t0:t0 + Tn], op=ALU.mult),
                                 reads=[PS[pb], 'rC'], writes=['t1'])
                            S.op('dve', lambda e, pb=pb: e.tensor_tensor(out=t2[:, :Tn], in0=ps[pb + 1][:, :Tn], in1=rS[:, t0:t0 + Tn], op=ALU.mult),
                                 reads=[PS[pb + 1], 'rS'], writes=['t2'])
                            S.op('pool', lambda e, s2=s2: e.tensor_tensor(out=strr[s2][:, :Tn], in0=t1[:, :Tn], in1=t2[:, :Tn], op=ALU.add),
                                 reads=['t1', 't2'], writes=[f'str{s2}'])
                            if isq:
                                dst = QRT[b][row0:row0 + 128, t0:t0 + Tn]
                            else:
                                dst = KTD[b][row0:row0 + 128, CS + t0:CS + t0 + Tn]
                            S.dma('sp', dst, strr[s2][:, :Tn], reads=[f'str{s2}'], sem=f'str{s2}')
                    ns = Tn // 128
                    vs = sl
                    for s in range(ns):
                        for n0 in range(0, nv, 512):
                            nn = min(512, nv - n0)
                            pb = 4 + ((s * 2 + n0 // 512) % 2)
                            for k in range(8):
                                S.op('pe', lambda e, k=k, s=s, n0=n0, nn=nn, pb=pb: e.matmul(
                                    ps[pb][:, :nn], aT[:, k, s * 128:(s + 1) * 128], wv[:, k, n0:n0 + nn], start=(k == 0), stop=(k == 7)),
                                    reads=['wv', an], writes=[PS[pb]])
                            if (s + n0 // 512) % 2 == 0:
                                S.op('act', lambda e, s=s, n0=n0, nn=nn, pb=pb: e.activation(out=vst[vs][:, s, n0:n0 + nn], in_=ps[pb][:, :nn], func=AF.Copy),
                                     reads=[PS[pb]], writes=[f'vst{vs}'])
                            else:
                                S.op('dve', lambda e, s=s, n0=n0, nn=nn, pb=pb: e.tensor_copy(out=vst[vs][:, s, n0:n0 + nn], in_=ps[pb][:, :nn]),
                                     reads=[PS[pb]], writes=[f'vst{vs}'])
                    kb = keybase(seg) + t0
                    S.dma('pool', VD[b][kb:kb + Tn, 0:nv].rearrange("(s p) c -> p s c", p=128), vst[vs][:, :ns, :],
                          reads=[f'vst{vs}'], sem=f'stv{vs}')
            S.barrier()

    def phase_proj_mla(i, need_ctx):
        with ExitStack() as ph:
            wcq = sb(ph, 'wcq', (128, 8, 384), BF16)
            wckv = sb(ph, 'wckv', (128, 8, 256), BF16)
            wkpe = sb(ph, 'wkpe', (128, 8, 192), BF16)
            wqb = sb(ph, 'wqb', (128, 3, 3072), BF16)
            wkn = sb(ph, 'wkn', (128, 2, 1024), BF16)
            wv = sb(ph, 'wv', (128, 2, 1024), BF16)
            rC = sb(ph, 'rC', (128, LS))
            rS = sb(ph, 'rS', (128, LS))
            qn = sb(ph, 'qn', (128, 3))
            kvn = sb(ph, 'kvn', (128, 2))
            for t, k in ((wcq, 'mla_cq'), (wckv, 'mla_ckv'), (wkpe, 'mla_kpe'), (wqb, 'mla_qb'), (wkn, 'mla_kn'), (wv, 'mla_v')):
                S.dma('sp', t[:], chunked(Wb[k]), writes=['wts'], sem='ldw')
            for q0 in range(0, LS, 1024):
                S.dma('sp', rC[:, q0:q0 + 1024], ropeC32_in[:, q0:q0 + 1024], writes=['rC'], sem='ldw')
                S.dma('sp', rS[:, q0:q0 + 1024], ropeS32_in[:, q0:q0 + 1024], writes=['rS'], sem='ldw')
            with nc.allow_non_contiguous_dma("tiny transposed vector loads"):
                S.dma('sp', qn[:], qn_in[0, :].rearrange("(o p) -> p o", p=128), writes=['wts'], sem='ldw')
                S.dma('sp', kvn[:], kvn_in[0, :].rearrange("(o p) -> p o", p=128), writes=['wts'], sem='ldw')
            hin = [sb(ph, f'hin{k}', (128, 8, 512)) for k in range(2)]
            aTs = [sb(ph, f'aT{k}', (128, 8, 512), BF16) for k in range(2)]
            cf = sb(ph, 'cf', (128, 5, 512))
            sq = [sb(ph, f'sq{k}', (128, 512)) for k in range(2)]
            rstd = sb(ph, 'rstd', (128, 2, 512))
            cn = sb(ph, 'cn', (128, 5, 512), BF16)
            stp = [sb(ph, f'stp{k}', (128, 512), BF16) for k in range(2)]
            strr = [sb(ph, f'str{k}', (128, 512), BF16) for k in range(2)]
            t1 = sb(ph, 't1', (128, 512))
            t2 = sb(ph, 't2', (128, 512))
            vst = [sb(ph, f'vst{k}', (128, 4, 1024), BF16) for k in range(2)]
            n = 0
            cnt = dict(p=0, r=0, sq=0, pb=0)

            def plain_store(pbi, rows, Tn, dst):
                s2 = cnt['p'] % 2
                cnt['p'] += 1
                S.op('act', lambda e: e.activation(out=stp[s2][:rows, :Tn], in_=ps[pbi][:rows, :Tn], func=AF.Copy),
                     reads=[PS[pbi]], writes=[f'stp{s2}'])
                S.dma('pool', dst, stp[s2][:rows, :Tn], reads=[f'stp{s2}'], sem=f'stp{s2}')

            for (b, seg, Ls, jm) in segs_for(need_ctx):
                Tn = min(512, Ls)
                lat = seg == 'L'
                for t0 in range(0, Ls, Tn):
                    sl = n % 2
                    n += 1
                    load_mod(hin, (aTs[sl], f'aT{sl}'), sl, b, seg, t0, Tn, jm, 0)
                    aT = aTs[sl]
                    an = f'aT{sl}'
                    for c5 in range(5):
                        wsrc = wcq if c5 < 3 else wckv
                        cc = c5 if c5 < 3 else c5 - 3
                        pb = c5 % 2
                        for k in range(8):
                            S.op('pe', lambda e, k=k, cc=cc, pb=pb, wsrc=wsrc: e.matmul(
                                ps[pb][:, :Tn], wsrc[:, k, cc * 128:(cc + 1) * 128], aT[:, k, :Tn], start=(k == 0), stop=(k == 7)),
                                reads=['wts', an], writes=[PS[pb]])
                        S.op('act', lambda e, c5=c5, pb=pb: e.activation(out=cf[:, c5, :Tn], in_=ps[pb][:, :Tn], func=AF.Copy),
                             reads=[PS[pb]], writes=['cf'])
                        s2 = cnt['sq'] % 2
                        cnt['sq'] += 1
                        S.op('act', lambda e, c5=c5, s2=s2: e.activation(out=sq[s2][:, :Tn], in_=cf[:, c5, :Tn], func=AF.Square),
                             reads=['cf'], writes=[f'sq{s2}'])
                        grp = 0 if c5 < 3 else 1
                        first = c5 in (0, 3)
                        last = c5 in (2, 4)
                        S.op('pe', lambda e, s2=s2, grp=grp, first=first, last=last: e.matmul(
                            ps[2 + grp][:, :Tn], onesf[:, :], sq[s2][:, :Tn], start=first, stop=last),
                            reads=[f'sq{s2}', 'onesf'], writes=[PS[2 + grp]])
                    for grp, dim in ((0, 384.0), (1, 256.0)):
                        S.op('act', lambda e, grp=grp, dim=dim: e.activation(
                            out=rstd[:, grp, :Tn], in_=ps[2 + grp][:, :Tn], func=AF.Sqrt, bias=epsrms[:, 0:1], scale=1.0 / dim),
                            reads=[PS[2 + grp]], writes=['rstd'])
                        S.op('dve', lambda e, grp=grp: e.reciprocal(out=rstd[:, grp, :Tn], in_=rstd[:, grp, :Tn]),
                            reads=['rstd'], writes=['rstd'])
                    for c5 in range(5):
                        grp = 0 if c5 < 3 else 1
                        gv = qn[:, c5:c5 + 1] if c5 < 3 else kvn[:, c5 - 3:c5 - 2]
                        S.op('dve', lambda e, c5=c5, grp=grp, gv=gv: e.scalar_tensor_tensor(
                            out=cn[:, c5, :Tn], in0=cf[:, c5, :Tn], scalar=gv, in1=rstd[:, grp, :Tn], op0=ALU.mult, op1=ALU.mult),
                            reads=['cf', 'rstd', 'wts'], writes=['cn'])
                    for k in range(8):
                        S.op('pe', lambda e, k=k: e.matmul(ps[4][:96, :Tn], wkpe[:, k, 0:96], aT[:, k, :Tn], start=(k == 0), stop=(k == 7)),
                             reads=['wts', an], writes=[PS[4]])
                    if lat:
                        for k in range(8):
                            S.op('pe', lambda e, k=k: e.matmul(ps[5][:96, :Tn], wkpe[:, k, 96:192], aT[:, k, :Tn], start=(k == 0), stop=(k == 7)),
                                 reads=['wts', an], writes=[PS[5]])
                        s2 = cnt['r'] % 2
                        cnt['r'] += 1
                        S.op('dve', lambda e: e.tensor_tensor(out=t1[64:96, :Tn], in0=ps[4][64:96, :Tn], in1=rC[64:96, t0:t0 + Tn], op=ALU.mult),
                             reads=[PS[4], 'rC'], writes=['t1'])
                        S.op('dve', lambda e: e.tensor_tensor(out=t2[64:96, :Tn], in0=ps[5][64:96, :Tn], in1=rS[64:96, t0:t0 + Tn], op=ALU.mult),
                             reads=[PS[5], 'rS'], writes=['t2'])
                        S.op('pool', lambda e, s2=s2: e.tensor_tensor(out=strr[s2][64:96, :Tn], in0=t1[64:96, :Tn], in1=t2[64:96, :Tn], op=ALU.add),
                             reads=['t1', 't2'], writes=[f'str{s2}'])
                        S.dma('sp', KPE[b][:, CS + t0:CS + t0 + Tn], strr[s2][64:96, :Tn], reads=[f'str{s2}'], sem=f'str{s2}')
                    else:
                        s2 = cnt['p'] % 2
                        cnt['p'] += 1
                        S.op('act', lambda e, s2=s2: e.activation(out=stp[s2][64:96, :Tn], in_=ps[4][64:96, :Tn], func=AF.Copy),
                             reads=[PS[4]], writes=[f'stp{s2}'])
                        S.dma('pool', KPE[b][:, t0:t0 + Tn], stp[s2][64:96, :Tn], reads=[f'stp{s2}'], sem=f'stp{s2}')
                    for h in range(16):
                        pb = 4 + (h % 2) * 2
                        for c in range(3):
                            S.op('pe', lambda e, c=c, h=h, pb=pb: e.matmul(
                                ps[pb][:96, :Tn], wqb[:, c, h * 96:(h + 1) * 96], cn[:, c, :Tn], start=(c == 0), stop=(c == 2)),
                                reads=['wts', 'cn'], writes=[PS[pb]])
                        if lat:
                            for c in range(3):
                                S.op('pe', lambda e, c=c, h=h, pb=pb: e.matmul(
                                    ps[pb + 1][:96, :Tn], wqb[:, c, 1536 + h * 96:1536 + (h + 1) * 96], cn[:, c, :Tn], start=(c == 0), stop=(c == 2)),
                                    reads=['wts', 'cn'], writes=[PS[pb + 1]])
                        if not lat:
                            plain_store(pb, 96, Tn, QPT[b][h * 96:(h + 1) * 96, colbase(seg) + t0:colbase(seg) + t0 + Tn])
                        if lat:
                            s3 = cnt['p'] % 2
                            cnt['p'] += 1
                            S.op('dve', lambda e, pb=pb, s3=s3: e.tensor_copy(out=stp[s3][:96, :Tn], in_=ps[pb][:96, :Tn]),
                                 reads=[PS[pb]], writes=[f'stp{s3}'])
                            S.dma('pool', QPT[b][h * 96:(h + 1) * 96, colbase(seg) + t0:colbase(seg) + t0 + Tn], stp[s3][:96, :Tn],
                                  reads=[f'stp{s3}'], sem=f'stp{s3}')
                            s2 = cnt['r'] % 2
                            cnt['r'] += 1
                            S.op('dve', lambda e, pb=pb, s2=s2: e.tensor_copy(out=strr[s2][0:64, :Tn], in_=ps[pb][0:64, :Tn]),
                                 reads=[PS[pb]], writes=[f'str{s2}'])
                            S.op('dve', lambda e, pb=pb: e.tensor_tensor(out=t1[64:96, :Tn], in0=ps[pb][64:96, :Tn], in1=rC[64:96, t0:t0 + Tn], op=ALU.mult),
                                 reads=[PS[pb], 'rC'], writes=['t1'])
                            S.op('dve', lambda e, pb=pb: e.tensor_tensor(out=t2[64:96, :Tn], in0=ps[pb + 1][64:96, :Tn], in1=rS[64:96, t0:t0 + Tn], op=ALU.mult),
                                 reads=[PS[pb + 1], 'rS'], writes=['t2'])
                            S.op('pool', lambda e, s2=s2: e.tensor_tensor(out=strr[s2][64:96, :Tn], in0=t1[64:96, :Tn], in1=t2[64:96, :Tn], op=ALU.add),
                                 reads=['t1', 't2'], writes=[f'str{s2}'])
                            S.dma('sp', QRT[b][h * 96:(h + 1) * 96, t0:t0 + Tn], strr[s2][0:96, :Tn], reads=[f'str{s2}'], sem=f'str{s2}')
                    for oc in range(8):
                        pb = oc % 2
                        for c in range(2):
                            S.op('pe', lambda e, c=c, oc=oc, pb=pb: e.matmul(
                                ps[pb][:, :Tn], wkn[:, c, oc * 128:(oc + 1) * 128], cn[:, 3 + c, :Tn], start=(c == 0), stop=(c == 1)),
                                reads=['wts', 'cn'], writes=[PS[pb]])
                        plain_store(pb, 128, Tn, KTD[b][oc * 128:(oc + 1) * 128, keybase(seg) + t0:keybase(seg) + t0 + Tn])
                    ns = Tn // 128
                    vs = sl
                    for s in range(ns):
                        for n0 in (0, 512):
                            pb = 2 + (n0 // 512)
                            for c in range(2):
                                S.op('pe', lambda e, c=c, s=s, n0=n0, pb=pb: e.matmul(
                                    ps[pb][:, :512], cn[:, 3 + c, s * 128:(s + 1) * 128], wv[:, c, n0:n0 + 512], start=(c == 0), stop=(c == 1)),
                                    reads=['wts', 'cn'], writes=[PS[pb]])
                            if n0 == 0:
                                S.op('act', lambda e, s=s, n0=n0, pb=pb: e.activation(out=vst[vs][:, s, n0:n0 + 512], in_=ps[pb][:, :512], func=AF.Copy),
                                     reads=[PS[pb]], writes=[f'vst{vs}'])
                            else:
                                S.op('dve', lambda e, s=s, n0=n0, pb=pb: e.tensor_copy(out=vst[vs][:, s, n0:n0 + 512], in_=ps[pb][:, :512]),
                                     reads=[PS[pb]], writes=[f'vst{vs}'])
                    kb = keybase(seg) + t0
                    S.dma('pool', VD[b][kb:kb + Tn, :].rearrange("(s p) c -> p s c", p=128), vst[vs][:, :ns, :],
                          reads=[f'vst{vs}'], sem=f'stv{vs}')
            S.barrier()

    def attn_core(jobs, dq, scale, nblk, out_f32, bias_tabs=None, esink=None):
        with ExitStack() as ph:
            vw = 65 if nblk == 1 else 129
            ktb = [sb(ph, f'ktb{k}', (128, NK), BF16) for k in range(2)]
            vaug = [sb(ph, f'vaug{k}', (128, NK // 128, vw), BF16) for k in range(2)]
            qrb = [sb(ph, f'qrb{k}', (128, 512), BF16) for k in range(2)]
            qpb = [sb(ph, f'qpb{k}', (128, 512), BF16) for k in range(2)]
            pT = [sb(ph, f'pT{k}', (128, 512), BF16) for k in range(3)]
            sbs = [sb(ph, f'sbs{k}', (128, 512)) for k in range(2)]
            rec = sb(ph, 'rec', (128, 512))
            bcs = sb(ph, 'bcs', (128, 512))
            odt = F32 if out_f32 else BF16
            ost = [[sb(ph, f'ost{k}_{j}', (64, 512), odt) for j in range(nblk)] for k in range(2)]
            for k in range(2):
                S.op('dve', lambda e, k=k: e.memset(vaug[k][:, :, 64:65], 1.0), writes=[f'vaug{k}'])
            state = dict(kslot=-1, kkey=None, nload=0)
            loaded = {}

            def load_job(J, jidx):
                if J['kkey'] != state['kkey']:
                    state['kslot'] = (state['kslot'] + 1) % 2
                    state['kkey'] = J['kkey']
                    ks = state['kslot']
                    for (src, p0, rows) in J['ksrcs']:
                        S.dma('sp', ktb[ks][p0:p0 + rows, :], src, writes=[f'ktb{ks}'], sem=f'ldk{ks}')
                    for (src, c0) in J['vsrcs']:
                        S.dma('sp', vaug[ks][:, :, c0:c0 + 64], src.rearrange("(t p) c -> p t c", p=128), writes=[f'vaug{ks}'], sem=f'ldk{ks}')
                qs = jidx % 2
                if J['qr'] is not None:
                    S.dma('sp', qrb[qs][:dq, :J['N']], J['qr'], writes=[f'qrb{qs}'], sem=f'ldq{qs}')
                S.dma('sp', qpb[qs][:dq, :J['N']], J['qp'], writes=[f'qpb{qs}'], sem=f'ldq{qs}')
                loaded[jidx] = (state['kslot'], qs)

            def finish(J, jidx):
                N = J['N']
                po = 3 + (jidx % 2) * 2
                qs = jidx % 2
                if J.get('sink') is not None:
                    h = J['sink']
                    S.op('dve', lambda e: e.tensor_scalar(out=rec[64:65, :N], in0=ps[po][64:65, :N], scalar1=esink[64:65, h:h + 1], scalar2=None, op0=ALU.add),
                         reads=[PS[po], 'esink'], writes=['rec'])
                    S.op('dve', lambda e: e.reciprocal(out=rec[64:65, :N], in_=rec[64:65, :N]), reads=['rec'], writes=['rec'])
                else:
                    S.op('dve', lambda e: e.reciprocal(out=rec[64:65, :N], in_=ps[po][64:65, :N]), reads=[PS[po]], writes=['rec'])
                S.op('pe', lambda e: e.matmul(ps[7][0:64, :N], onesf[64:65, 0:64], rec[64:65, :N], start=True, stop=True),
                     reads=['rec', 'onesf'], writes=[PS[7]])
                S.op('act', lambda e: e.activation(out=bcs[0:64, :N], in_=ps[7][0:64, :N], func=AF.Copy), reads=[PS[7]], writes=['bcs'])
                for j in range(nblk):
                    S.op('dve', lambda e, j=j: e.tensor_tensor(out=ost[qs][j][:, :N], in0=ps[po + j][0:64, :N], in1=bcs[0:64, :N], op=ALU.mult),
                         reads=[PS[po + j], 'bcs'], writes=[f'ost{qs}_{j}'])
                    S.dma('pool', J['outs'][j], ost[qs][j][:, :N], reads=[f'ost{qs}_{j}'], sem=f'sto{qs}_{j}')

            stream = []
            for jidx, J in enumerate(jobs):
                nt = len(J['tiles'])
                for ti, T in enumerate(J['tiles']):
                    stream.append((jidx, ti, nt, T))
            nS = len(stream)

            def emit_S(g):
                jidx, ti, nt, (kt, use, q0, q1, bid) = stream[g]
                if jidx not in loaded:
                    load_job(jobs[jidx], jidx)
                ks, qs = loaded[jidx]
                qb = qrb[qs] if use == 'r' else qpb[qs]
                qn_ = (f'qrb{qs}' if use == 'r' else f'qpb{qs}')
                pb = g % 3
                S.op('pe', lambda e: e.matmul(ps[pb][:, q0:q1], ktb[ks][:dq, kt * 128:(kt + 1) * 128], qb[:dq, q0:q1], start=True, stop=True),
                     reads=[f'ktb{ks}', qn_], writes=[PS[pb]])

            def emit_E(g):
                jidx, ti, nt, (kt, use, q0, q1, bid) = stream[g]
                pb = g % 3
                if bid is not None:
                    s2 = g % 2
                    S.op('dve', lambda e: e.scalar_tensor_tensor(out=sbs[s2][:, q0:q1], in0=ps[pb][:, q0:q1], scalar=scale,
                                                                 in1=bias_tabs[:, bid, q0:q1], op0=ALU.mult, op1=ALU.add),
                         reads=[PS[pb], 'btab'], writes=[f'sbs{s2}'])
                    S.op('act', lambda e: e.activation(out=pT[pb][:, q0:q1], in_=sbs[s2][:, q0:q1], func=AF.Exp),
                         reads=[f'sbs{s2}'], writes=[f'pT{pb}'])
                else:
                    S.op('act', lambda e: e.activation(out=pT[pb][:, q0:q1], in_=ps[pb][:, q0:q1], func=AF.Exp, scale=scale),
                         reads=[PS[pb]], writes=[f'pT{pb}'])

            def emit_PV(g):
                jidx, ti, nt, (kt, use, q0, q1, bid) = stream[g]
                ks, qs = loaded[jidx]
                pb = g % 3
                po = 3 + (jidx % 2) * 2
                for j in range(nblk):
                    c0, M = ((0, 65), (65, 64))[j]
                    S.op('pe', lambda e, j=j, c0=c0, M=M: e.matmul(ps[po + j][:M, q0:q1], vaug[ks][:, kt, c0:c0 + M], pT[pb][:, q0:q1],
                                                                   start=(ti == 0), stop=(ti == nt - 1)),
                         reads=[f'vaug{ks}', f'pT{pb}'], writes=[PS[po + j]])

            pending = None
            for g in range(nS):
                if g == 0:
                    emit_S(0)
                    if nS > 1:
                        emit_S(1)
                if g + 2 < nS:
                    emit_S(g + 2)
                emit_E(g)
                emit_PV(g)
                jidx, ti, nt, _ = stream[g]
                if pending is not None and (ti == min(1, nt - 1)):
                    finish(jobs[pending], pending)
                    pending = None
                if ti == nt - 1:
                    if pending is not None:
                        finish(jobs[pending], pending)
                    pending = jidx
            if pending is not None:
                finish(jobs[pending], pending)
            S.barrier()

    def phase_attn(i, kind, need_ctx):
        with ExitStack() as ph:
            bias_tabs = None
            esink = None
            if kind == 'swa':
                bias_tabs = sb(ph, 'btab', (128, 6, 512))
                S.dma('sp', bias_tabs[:], swamask_in.rearrange("t p q -> p t q"), writes=['btab'], sem='ldw')
                esink = sb(ph, 'esink', (128, 16))
                S.dma('sp', esink[64:65, :], sinks_in[0:1, :], writes=['esink'], sem='ldw')
                S.op('act', lambda e: e.activation(out=esink[64:65, :], in_=esink[64:65, :], func=AF.Exp), reads=['esink'], writes=['esink'])
            if kind == 'na':
                for h in range(16):
                    with ExitStack() as ph2:
                        bt = sb(ph2, 'btab', (128, 20, 512))
                        S.dma('sp', bt[:], nabias_in[h].rearrange("t p q -> p t q"), writes=['btab'], sem='ldw')
                        jobs = []
                        for b in range(NB):
                            ksrcs = [(KTD[b][h * 64:(h + 1) * 64, :], 0, 64)]
                            vsrcs = [(VD[b][:, h * 64:(h + 1) * 64], 0)]
                            for R in range(8):
                                tiles = [(0, 'p', 0, 512, None)] + [(2 + kt2, 'p', 0, 512, tid) for (kt2, tid) in _na_tile_list(R)] + [(1, 'p', 0, 512, None)]
                                jobs.append(dict(kkey=(b, h), ksrcs=ksrcs, vsrcs=vsrcs, qr=None, qp=QPT[b][h * 64:(h + 1) * 64, R * 512:(R + 1) * 512],
                                                 N=512, tiles=tiles, outs=[OTD[b][h * 64:(h + 1) * 64, R * 512:(R + 1) * 512]]))
                            if need_ctx:
                                jobs.append(dict(kkey=(b, h), ksrcs=ksrcs, vsrcs=vsrcs, qr=None, qp=QPT[b][h * 64:(h + 1) * 64, LS:LS + CS],
                                                 N=CS, tiles=[(0, 'p', 0, CS, None), (1, 'p', 0, CS, None)],
                                                 outs=[OTD[b][h * 64:(h + 1) * 64, LS:LS + CS]]))
                        attn_core(jobs, 64, 0.125, 1, False, bias_tabs=bt)
                return
            jobs = []
            if kind == 'mla':
                dq, scale, nblk, of32 = 96, 96 ** -0.5, 1, False
                for b in range(NB):
                    for h in range(16):
                        ksrcs = [(KTD[b][h * 64:(h + 1) * 64, :], 0, 64), (KPE[b][:, :], 64, 32)]
                        vsrcs = [(VD[b][:, h * 64:(h + 1) * 64], 0)]
                        tiles = [(0, 'p', 0, 512, None)] + [(2 + k, 'r', 0, 512, None) for k in range(32)] + [(1, 'p', 0, 512, None)]
                        for Qt in range(8):
                            jobs.append(dict(kkey=(b, h), ksrcs=ksrcs, vsrcs=vsrcs, qr=QRT[b][h * 96:(h + 1) * 96, Qt * 512:(Qt + 1) * 512],
                                             qp=QPT[b][h * 96:(h + 1) * 96, Qt * 512:(Qt + 1) * 512], N=512, tiles=tiles,
                                             outs=[OTD[b][h * 64:(h + 1) * 64, Qt * 512:(Qt + 1) * 512]]))
                        if need_ctx:
                            jobs.append(dict(kkey=(b, h), ksrcs=ksrcs, vsrcs=vsrcs, qr=None, qp=QPT[b][h * 96:(h + 1) * 96, LS:LS + CS],
                                             N=CS, tiles=[(0, 'p', 0, CS, None), (1, 'p', 0, CS, None)],
                                             outs=[OTD[b][h * 64:(h + 1) * 64, LS:LS + CS]]))
            elif kind == 'diff':
                dq, scale, nblk, of32 = 64, 0.125, 2, True
                for b in range(NB):
                    for hv in range(16):
                        h, j = hv // 2, hv % 2
                        ksrcs = [(KTD[b][hv * 64:(hv + 1) * 64, :], 0, 64)]
                        vsrcs = [(VD[b][:, h * 128:h * 128 + 64], 0), (VD[b][:, h * 128 + 64:h * 128 + 128], 65)]
                        tiles = [(0, 'p', 0, 512, None)] + [(2 + k, 'r', 0, 512, None) for k in range(32)] + [(1, 'p', 0, 512, None)]
                        for Qt in range(8):
                            jobs.append(dict(kkey=(b, hv), ksrcs=ksrcs, vsrcs=vsrcs, qr=QRT[b][hv * 64:(hv + 1) * 64, Qt * 512:(Qt + 1) * 512],
                                             qp=QPT[b][hv * 64:(hv + 1) * 64, Qt * 512:(Qt + 1) * 512], N=512, tiles=tiles,
                                             outs=[OD[b][j][h * 128:h * 128 + 64, Qt * 512:(Qt + 1) * 512],
                                                   OD[b][j][h * 128 + 64:h * 128 + 128, Qt * 512:(Qt + 1) * 512]]))
                        if need_ctx:
                            jobs.append(dict(kkey=(b, hv), ksrcs=ksrcs, vsrcs=vsrcs, qr=None, qp=QPT[b][hv * 64:(hv + 1) * 64, LS:LS + CS],
                                             N=CS, tiles=[(0, 'p', 0, CS, None), (1, 'p', 0, CS, None)],
                                             outs=[OD[b][j][h * 128:h * 128 + 64, LS:LS + CS], OD[b][j][h * 128 + 64:h * 128 + 128, LS:LS + CS]]))
            elif kind == 'swa':
                dq, scale, nblk, of32 = 64, 0.125, 1, False
                for b in range(NB):
                    for h in range(16):
                        g = h // 4
                        ksrcs = [(KTD[b][g * 64:(g + 1) * 64, :], 0, 64)]
                        vsrcs = [(VD[b][:, g * 64:(g + 1) * 64], 0)]
                        for Qt in range(8):
                            tiles = [(0, 'p', 0, 512, None)]
                            for kk in range(-1, 5):
                                kt2 = 4 * Qt + kk
                                if kt2 < 0 or kt2 > 31:
                                    continue
                                tiles.append((2 + kt2, 'r', max(0, 128 * kk - 128), min(512, 128 * kk + 256), kk + 1))
                            tiles.append((1, 'p', 0, 512, None))
                            jobs.append(dict(kkey=(b, g), ksrcs=ksrcs, vsrcs=vsrcs, qr=QRT[b][h * 64:(h + 1) * 64, Qt * 512:(Qt + 1) * 512],
                                             qp=QPT[b][h * 64:(h + 1) * 64, Qt * 512:(Qt + 1) * 512], N=512, tiles=tiles,
                                             outs=[OTD[b][h * 64:(h + 1) * 64, Qt * 512:(Qt + 1) * 512]], sink=h))
            attn_core(jobs, dq, scale, nblk, of32, bias_tabs=bias_tabs, esink=esink)

    def phase_diff_combine(i, need_ctx):
        lam_init = 0.8 - 0.6 * math.exp(-0.3 * i)
        with ExitStack() as ph:
            lt = sb(ph, 'lt', (1, 4, 64))
            pr = sb(ph, 'pr', (1, 2, 64))
            sm = sb(ph, 'sm', (1, 4))
            nlam = sb(ph, 'nlam', (128, 1))
            subs = sb(ph, 'subs', (128, 1))
            S.dma('sp', lt[:], lam_in[0:1, :, :], writes=['lt'], sem='ldw')
            with nc.allow_non_contiguous_dma("tiny transposed vector loads"):
                S.dma('sp', subs[:], subln_in[0, :].rearrange("(p o) -> p o", o=1), writes=['subs'], sem='ldw')
            S.op('dve', lambda e: e.tensor_tensor(out=pr[:, 0, :], in0=lt[:, 0, :], in1=lt[:, 1, :], op=ALU.mult), reads=['lt'], writes=['pr'])
            S.op('dve', lambda e: e.tensor_tensor(out=pr[:, 1, :], in0=lt[:, 2, :], in1=lt[:, 3, :], op=ALU.mult), reads=['lt'], writes=['pr'])
            S.op('dve', lambda e: e.reduce_sum(out=sm[:, 0:2], in_=pr[:], axis=mybir.AxisListType.X), reads=['pr'], writes=['sm'])
            S.op('act', lambda e: e.activation(out=sm[:, 0:2], in_=sm[:, 0:2], func=AF.Exp), reads=['sm'], writes=['sm'])
            S.op('dve', lambda e: e.tensor_tensor(out=sm[:, 2:3], in0=sm[:, 1:2], in1=sm[:, 0:1], op=ALU.subtract), reads=['sm'], writes=['sm'])
            S.op('dve', lambda e: e.tensor_scalar_add(out=sm[:, 2:3], in0=sm[:, 2:3], scalar1=-lam_init), reads=['sm'], writes=['sm'])
            S.op('pe', lambda e: e.matmul(ps[0][:, 0:1], onesf[0:1, :], sm[0:1, 2:3], start=True, stop=True), reads=['sm', 'onesf'], writes=[PS[0]])
            S.op('dve', lambda e: e.tensor_copy(out=nlam[:], in_=ps[0][:, 0:1]), reads=[PS[0]], writes=['nlam'])
            S.op('dve', lambda e: e.tensor_scalar(out=subs[:], in0=subs[:], scalar1=(1.0 - lam_init), scalar2=None, op0=ALU.mult), reads=['subs'], writes=['subs'])
            o1 = [sb(ph, f'o1_{k}', (128, 8, 512)) for k in range(2)]
            o2 = [sb(ph, f'o2_{k}', (128, 8, 512)) for k in range(2)]
            oo = sb(ph, 'oo', (128, 512))
            sq = sb(ph, 'sq', (128, 512))
            rs = sb(ph, 'rs', (128, 512))
            on = [sb(ph, f'on{k}', (128, 8, 512), BF16) for k in range(2)]
            n = 0
            for (b, seg, Ls, jm) in segs_for(need_ctx):
                Tn = min(512, Ls)
                for t0 in range(0, Ls, Tn):
                    sl = n % 2
                    n += 1
                    c0 = colbase(seg) + t0
                    S.dma('sp', o1[sl][:, :, :Tn], chunked(OD[b][0])[:, :, c0:c0 + Tn], writes=[f'o1_{sl}'], sem=f'ldo{sl}')
                    S.dma('sp', o2[sl][:, :, :Tn], chunked(OD[b][1])[:, :, c0:c0 + Tn], writes=[f'o2_{sl}'], sem=f'ldo{sl}')
                    for h in range(8):
                        pb = h % 2
                        S.op('dve', lambda e, h=h: e.scalar_tensor_tensor(out=oo[:, :Tn], in0=o2[sl][:, h, :Tn], scalar=nlam[:, 0:1], in1=o1[sl][:, h, :Tn],
                                                                          op0=ALU.mult, op1=ALU.add),
                             reads=[f'o1_{sl}', f'o2_{sl}', 'nlam'], writes=['oo'])
                        S.op('act', lambda e: e.activation(out=sq[:, :Tn], in_=oo[:, :Tn], func=AF.Square), reads=['oo'], writes=['sq'])
                        S.op('pe', lambda e, pb=pb: e.matmul(ps[pb][:, :Tn], onesf[:, :], sq[:, :Tn], start=True, stop=True),
                             reads=['sq', 'onesf'], writes=[PS[pb]])
                        S.op('act', lambda e, pb=pb: e.activation(out=rs[:, :Tn], in_=ps[pb][:, :Tn], func=AF.Sqrt, bias=epsrms[:, 0:1], scale=1.0 / 128.0),
                             reads=[PS[pb]], writes=['rs'])
                        S.op('dve', lambda e: e.reciprocal(out=rs[:, :Tn], in_=rs[:, :Tn]), reads=['rs'], writes=['rs'])
                        S.op('dve', lambda e, h=h: e.scalar_tensor_tensor(out=on[sl][:, h, :Tn], in0=oo[:, :Tn], scalar=subs[:, 0:1], in1=rs[:, :Tn],
                                                                          op0=ALU.mult, op1=ALU.mult),
                             reads=['oo', 'rs', 'subs'], writes=[f'on{sl}'])
                    S.dma('pool', chunked(OTD[b])[:, :, c0:c0 + Tn], on[sl][:, :, :Tn], reads=[f'on{sl}'], sem=f'ston{sl}')
            S.barrier()

    def phase_res_ln(i, which, need_ctx):
        KC = 8 if which == 1 else 22
        wname = f'wo_{i}' if which == 1 else f'w2_{i}'
        srcd = OTD if which == 1 else GD
        goff = 16 if which == 1 else 40
        lq = 0 if which == 1 else 2
        with ExitStack() as ph:
            W = sb(ph, 'Wres', (128, KC, 1024), BF16)
            S.dma('sp', W[:], chunked(Wb[wname]), writes=['Wres'], sem='ldw')
            srcb = [sb(ph, f'srcb{k}', (128, KC, 512), BF16) for k in range(2)]
            hb = [sb(ph, f'hb{k}', (128, 8, 512)) for k in range(2)]
            v = sb(ph, 'v', (128, 8, 512))
            gy = [sb(ph, f'gy{k}', (128, 512)) for k in range(2)]
            sq = [sb(ph, f'sq{k}', (128, 512)) for k in range(2)]
            mean = sb(ph, 'mean', (128, 512))
            var = sb(ph, 'var', (128, 512))
            rstd = sb(ph, 'rstd', (128, 512))
            tt = [sb(ph, f'tt{k}', (128, 512)) for k in range(2)]
            n = 0
            for (b, seg, Ls, jm) in segs_for(need_ctx):
                Tn = min(512, Ls)
                for t0 in range(0, Ls, Tn):
                    sl = n % 2
                    n += 1
                    c0 = colbase(seg) + t0
                    S.dma('sp', srcb[sl][:, :, :Tn], chunked(srcd[b])[:, :, c0:c0 + Tn], writes=[f'srcb{sl}'], sem=f'lds{sl}')
                    S.dma('sp', hb[sl][:, :, :Tn], hview(b, seg)[:, :, t0:t0 + Tn], writes=[f'hb{sl}'], sem=f'ldh{sl}')

                    def emit_Y(oc):
                        pb = oc % 2
                        for k in range(KC):
                            S.op('pe', lambda e, k=k, oc=oc, pb=pb: e.matmul(ps[pb][:, :Tn], W[:, k, oc * 128:(oc + 1) * 128], srcb[sl][:, k, :Tn],
                                                                             start=(k == 0), stop=(k == KC - 1)),
                                 reads=['Wres', f'srcb{sl}'], writes=[PS[pb]])
                        S.op('act', lambda e, oc=oc, pb=pb: e.activation(out=gy[pb][:, :Tn], in_=ps[pb][:, :Tn], func=AF.Copy,
                                                                         scale=mod[:, jm, goff + oc:goff + oc + 1]),
                             reads=[PS[pb], 'mod'], writes=[f'gy{pb}'])
                        S.op('dve', lambda e, oc=oc, pb=pb: e.scalar_tensor_tensor(out=v[:, oc, :Tn], in0=hb[sl][:, oc, :Tn], scalar=ALPHA, in1=gy[pb][:, :Tn],
                                                                                  op0=ALU.mult, op1=ALU.add),
                             reads=[f'hb{sl}', f'gy{pb}'], writes=['v'])
                        S.op('act', lambda e, oc=oc, pb=pb: e.activation(out=sq[pb][:, :Tn], in_=v[:, oc, :Tn], func=AF.Square),
                             reads=['v'], writes=[f'sq{pb}'])

                    def emit_ST(oc):
                        pb = oc % 2
                        S.op('pe', lambda e, oc=oc: e.matmul(ps[2][:, :Tn], onesf[:, :], v[:, oc, :Tn], start=(oc == 0), stop=(oc == 7)),
                             reads=['v', 'onesf'], writes=[PS[2]])
                        S.op('pe', lambda e, oc=oc, pb=pb: e.matmul(ps[3][:, :Tn], onesf[:, :], sq[pb][:, :Tn], start=(oc == 0), stop=(oc == 7)),
                             reads=[f'sq{pb}', 'onesf'], writes=[PS[3]])

                    emit_Y(0)
                    for oc in range(8):
                        if oc + 1 < 8:
                            emit_Y(oc + 1)
                        emit_ST(oc)
                    S.op('dve', lambda e: e.tensor_scalar(out=mean[:, :Tn], in0=ps[2][:, :Tn], scalar1=1.0 / D, scalar2=None, op0=ALU.mult),
                         reads=[PS[2]], writes=['mean'])
                    S.op('dve', lambda e: e.tensor_tensor(out=var[:, :Tn], in0=mean[:, :Tn], in1=mean[:, :Tn], op=ALU.mult), reads=['mean'], writes=['var'])
                    S.op('dve', lambda e: e.scalar_tensor_tensor(out=var[:, :Tn], in0=ps[3][:, :Tn], scalar=1.0 / D, in1=var[:, :Tn], op0=ALU.mult, op1=ALU.subtract),
                         reads=[PS[3], 'var'], writes=['var'])
                    S.op('act', lambda e: e.activation(out=rstd[:, :Tn], in_=var[:, :Tn], func=AF.Sqrt, bias=epsln[:, 0:1], scale=1.0),
                         reads=['var'], writes=['rstd'])
                    S.op('dve', lambda e: e.reciprocal(out=rstd[:, :Tn], in_=rstd[:, :Tn]), reads=['rstd'], writes=['rstd'])
                    for oc in range(8):
                        k2 = oc % 2
                        S.op('dve', lambda e, oc=oc, k2=k2: e.tensor_tensor(out=tt[k2][:, :Tn], in0=v[:, oc, :Tn], in1=mean[:, :Tn], op=ALU.subtract),
                             reads=['v', 'mean'], writes=[f'tt{k2}'])
                        S.op('dve', lambda e, oc=oc, k2=k2: e.tensor_tensor(out=tt[k2][:, :Tn], in0=tt[k2][:, :Tn], in1=rstd[:, :Tn], op=ALU.mult),
                             reads=[f'tt{k2}', 'rstd'], writes=[f'tt{k2}'])
                        S.op('act', lambda e, oc=oc, k2=k2: e.activation(out=hb[sl][:, oc, :Tn], in_=tt[k2][:, :Tn], func=AF.Identity,
                                                                         bias=lnp[:, lq + 1, oc:oc + 1], scale=lnp[:, lq, oc:oc + 1]),
                             reads=[f'tt{k2}', 'lnp'], writes=[f'hb{sl}'])
                    S.dma('pool', hview(b, seg)[:, :, t0:t0 + Tn], hb[sl][:, :, :Tn], reads=[f'hb{sl}'], sem=f'sth{sl}')
            S.barrier()

    def phase_ffn1(i, need_ctx):
        import os as _os
        fdbg = int(_os.environ.get('FDBG', '0'))
        with ExitStack() as ph:
            aTr = sb(ph, 'aTr', (128, 8, LS), BF16)
            hin = [sb(ph, f'hin{k}', (128, 8, 256)) for k in range(2)]
            ua = [sb(ph, f'ua{k}', (128, 2050)) for k in range(2)]
            ug = [sb(ph, f'ug{k}', (128, 2050)) for k in range(2)]
            ca = sb(ph, 'ca', (128, 2048))
            cg = sb(ph, 'cg', (128, 2048))
            gst = [sb(ph, f'gst{k}', (128, 2048), BF16) for k in range(2)]
            wst = [sb(ph, f'wst{k}', (128, 8, 256), BF16) for k in range(2)]
            cw = sb(ph, 'cw', (128, 3, 44))
            cb = sb(ph, 'cb', (128, 44))
            with nc.allow_non_contiguous_dma("tiny transposed vector loads"):
                if not (fdbg & 4):
                    for t in range(3):
                        S.dma('sp', cw[:, t, :], convw_in[i, t, :].rearrange("(c p) -> p c", p=128), writes=['cw'], sem='ldw')
                    S.dma('sp', cb[:], convb_in[i, :].rearrange("(c p) -> p c", p=128), writes=['cw'], sem='ldw')
            w1c = chunked(Wb[f'w1_{i}'])
            nw = 0
            ng = 0
            nh_ = 0
            fseg = int(_os.environ.get('FSEG', '99'))
            fch = int(_os.environ.get('FCH', '22'))
            for (b, seg, Ls, jm) in segs_for(need_ctx)[:fseg]:
                for t0 in range(0, Ls, 256):
                    sl = nh_ % 2
                    nh_ += 1
                    load_mod(hin, (aTr, 'aTr'), sl, b, seg, t0, 256, jm, 1, acols=t0)
                nhalf = 2 if Ls == LS else 1
                HL = Ls // nhalf
                Tn = min(512, Ls)
                for u, un in ((ua, 'ua'), (ug, 'ug')):
                    S.op('dve', lambda e, u=u: e.memset(u[0][:, 0:1], 0.0), writes=[f'{un}0'])
                    S.op('dve', lambda e, u=u: e.memset(u[nhalf - 1][:, HL + 1:HL + 2], 0.0), writes=[f'{un}{nhalf - 1}'])
                for c in range(fch):
                    ws = nw % 2
                    nw += 1
                    S.dma('sp', wst[ws][:, :, 0:128], w1c[:, :, c * 128:(c + 1) * 128], writes=[f'wst{ws}'], sem=f'ldw1{ws}')
                    S.dma('sp', wst[ws][:, :, 128:256], w1c[:, :, FH + c * 128:FH + (c + 1) * 128], writes=[f'wst{ws}'], sem=f'ldw1{ws}')
                    for t0 in range(0, Ls, Tn):
                        hf = t0 // HL
                        j0 = t0 - hf * HL + 1
                        for half, u, un in ((0, ua, 'ua'), (1, ug, 'ug')):
                            pb = ((t0 // Tn) % 2) * 2 + half
                            for k in range(8):
                                S.op('pe', lambda e, k=k, half=half, pb=pb: e.matmul(ps[pb][:, :Tn], wst[ws][:, k, half * 128:(half + 1) * 128], aTr[:, k, t0:t0 + Tn],
                                                                                     start=(k == 0), stop=(k == 7)),
                                     reads=[f'wst{ws}', 'aTr'], writes=[PS[pb]])
                            eng = 'act' if half == 0 else 'dve'
                            if eng == 'act':
                                S.op('act', lambda e, u=u, pb=pb, hf=hf, j0=j0: e.activation(out=u[hf][:, j0:j0 + Tn], in_=ps[pb][:, :Tn], func=AF.Copy),
                                     reads=[PS[pb]], writes=[f'{un}{hf}'])
                            else:
                                S.op('dve', lambda e, u=u, pb=pb, hf=hf, j0=j0: e.tensor_copy(out=u[hf][:, j0:j0 + Tn], in_=ps[pb][:, :Tn]),
                                     reads=[PS[pb]], writes=[f'{un}{hf}'])
                            if nhalf == 2 and t0 + Tn == HL:
                                if eng == 'act':
                                    S.op('act', lambda e, u=u, pb=pb: e.activation(out=u[1][:, 0:1], in_=ps[pb][:, Tn - 1:Tn], func=AF.Copy),
                                         reads=[PS[pb]], writes=[f'{un}1'])
                                else:
                                    S.op('dve', lambda e, u=u, pb=pb: e.tensor_copy(out=u[1][:, 0:1], in_=ps[pb][:, Tn - 1:Tn]),
                                         reads=[PS[pb]], writes=[f'{un}1'])
                            if nhalf == 2 and t0 == HL:
                                if eng == 'act':
                                    S.op('act', lambda e, u=u, pb=pb: e.activation(out=u[0][:, HL + 1:HL + 2], in_=ps[pb][:, 0:1], func=AF.Copy),
                                         reads=[PS[pb]], writes=[f'{un}0'])
                                else:
                                    S.op('dve', lambda e, u=u, pb=pb: e.tensor_copy(out=u[0][:, HL + 1:HL + 2], in_=ps[pb][:, 0:1]),
                                         reads=[PS[pb]], writes=[f'{un}0'])
                    for hf in range(nhalf):
                        gs = ng % 2
                        ng += 1
                        for u, un, cv, cvn, ch in ((ua, 'ua', ca, 'ca', c), (ug, 'ug', cg, 'cg', 22 + c)):
                            S.op('act', lambda e, u=u, cv=cv, ch=ch, hf=hf: e.activation(out=cv[:, :HL], in_=u[hf][:, 1:HL + 1], func=AF.Identity,
                                                                                  bias=cb[:, ch:ch + 1], scale=cw[:, 1, ch:ch + 1]),
                                 reads=[f'{un}{hf}', 'cw'], writes=[cvn])
                            S.op('dve', lambda e, u=u, cv=cv, ch=ch, hf=hf: e.scalar_tensor_tensor(out=cv[:, :HL], in0=u[hf][:, 0:HL], scalar=cw[:, 0, ch:ch + 1], in1=cv[:, :HL],
                                                                                            op0=ALU.mult, op1=ALU.add),
                                 reads=[f'{un}{hf}', 'cw', cvn], writes=[cvn])
                            S.op('dve', lambda e, u=u, cv=cv, ch=ch, hf=hf: e.scalar_tensor_tensor(out=cv[:, :HL], in0=u[hf][:, 2:HL + 2], scalar=cw[:, 2, ch:ch + 1], in1=cv[:, :HL],
                                                                                            op0=ALU.mult, op1=ALU.add),
                                 reads=[f'{un}{hf}', 'cw', cvn], writes=[cvn])
                        S.op('act', lambda e: e.activation(out=cg[:, :HL], in_=cg[:, :HL], func=AF.Silu), reads=['cg'], writes=['cg'])
                        S.op('pool', lambda e, gs=gs: e.tensor_tensor(out=gst[gs][:, :HL], in0=ca[:, :HL], in1=cg[:, :HL], op=ALU.mult),
                             reads=['ca', 'cg'], writes=[f'gst{gs}'])
                        c0 = colbase(seg) + hf * HL
                        if not (fdbg & 1):
                            for q0 in range(0, HL, 512):
                                qn = min(512, HL - q0)
                                S.dma('sp', GD[b][c * 128:(c + 1) * 128, c0 + q0:c0 + q0 + qn], gst[gs][:, q0:q0 + qn], reads=[f'gst{gs}'], sem=f'stg{gs}')
            S.barrier()

    transpose_in()
    kinds = ['mla', 'na', 'diff', 'swa']
    for i in layers:
        need_ctx = (i != DEPTH - 1)
        kind = kinds[i % 4]
        import os as _os
        kstop = int(_os.environ.get('KSTOP', '99'))
        if kstop >= 1:
            phase_mod(i)
        if kstop >= 2:
            if kind == 'mla':
                phase_proj_mla(i, True)
            else:
                phase_proj(i, kind, True)
        if kstop >= 3:
            phase_attn(i, kind, need_ctx)
            if kind == 'diff':
                phase_diff_combine(i, need_ctx)
        if kstop >= 4:
            phase_res_ln(i, 1, need_ctx)
        if kstop >= 5:
            phase_ffn1(i, need_ctx)
        if kstop >= 6:
            phase_res_ln(i, 2, need_ctx)
    transpose_out()
    S.barrier()
    build.ninstr = S.ninstr
    build.trace = S.trace
    return nc, es


_CONSTS = {}


def _consts():
    if not _CONSTS:
        c64, s64 = _rope_tables(64, 0)
        c32, s32 = _rope_tables(32, 64)
        _CONSTS.update(ident=np.eye(128, dtype=np.float32), ropeC64=c64, ropeS64=s64, ropeC32=c32, ropeS32=s32,
                       swamask=_swa_masks())
    return _CONSTS


def make_in_maps(inp, ncores=NCORES):
    W = _prep_weights(inp)
    shared = dict(_consts())
    shared['nabias'] = _na_tables(np.asarray(inp['na_rpb'][0], np.float32))
    for k in ('c_ctx', 'ada_w', 'ada_b', 'ln1_g', 'ln1_b', 'ln2_g', 'ln2_b', 'mla_q_norm', 'mla_kv_norm',
              'diff_lambda', 'diff_subln', 'swa_sinks'):
        shared[k] = np.ascontiguousarray(inp[k], dtype=np.float32)
    shared['ffn_conv_w'] = np.ascontiguousarray(inp['ffn_conv_w'], dtype=np.float32)
    shared['ffn_conv_b'] = np.ascontiguousarray(inp['ffn_conv_b'], dtype=np.float32)
    for k, v in W.items():
        shared['wf_' + k] = v
    maps = []
    for core in range(ncores):
        m = dict(shared)
        m['x'] = np.ascontiguousarray(inp['x'][core * NB:(core + 1) * NB], dtype=np.float32)
        m['ctx'] = np.ascontiguousarray(inp['ctx'][core * NB:(core + 1) * NB], dtype=np.float32)
        m['c'] = np.ascontiguousarray(inp['c'][core * NB:(core + 1) * NB], dtype=np.float32)
        maps.append(m)
    return maps


def kernel(**inputs):
    inp = {k: np.asarray(v) for k, v in inputs.items()}
    nc, es = build()
    maps = make_in_maps(inp)
    res = run_bass_kernel_spmd(nc, maps, core_ids=list(range(NCORES)))
    es.close()
    out = np.concatenate([np.asarray(r['out'], dtype=np.float32) for r in res.results], axis=0)
    return out
```
